# Optimizing a Trainium2 kernel written in Bass

```python
import jax, jax.numpy as jnp
from jax import lax
import numpy as np

D_MODEL = 1024
BATCH = 4
SEQ = 8192
DEPTH = 1

CHUNK = 64
N_MEM = 256
MEM_HEADS = 4
MEM_HEAD_DIM = D_MODEL // MEM_HEADS
MIX_WIDTH = D_MODEL
HG_WIDTH = MIX_WIDTH // 2
HG_HEADS = 4
HG_HEAD_DIM = HG_WIDTH // HG_HEADS
SB_WIDTH = MIX_WIDTH - HG_WIDTH
SB_HEADS = 8
SB_HEAD_DIM = SB_WIDTH // SB_HEADS
SB_BLOCK = 128
D_FF = 2816
IN_COLS = 4 * HG_WIDTH + 3 * SB_WIDTH
EPS = 1e-6

kernel_name = "hybrid_hgrn2_stickbreaking_macaron_layer"


def rmsnorm(x, g):
    xf = x.astype(jnp.float32)
    y = xf * lax.rsqrt(jnp.mean(xf * xf, axis=-1, keepdims=True) + EPS)
    return (y * g.astype(jnp.float32)).astype(x.dtype)


def head_rmsnorm(o, g):
    B, S, H, d = o.shape
    of = o.astype(jnp.float32)
    y = of * lax.rsqrt(jnp.mean(of * of, axis=-1, keepdims=True) + EPS)
    return y * g.astype(jnp.float32).reshape(H, d)


def swiglu(h, w_gu, w_down):
    gu = h @ w_gu
    gate, up = jnp.split(gu, 2, axis=-1)
    return (jax.nn.silu(gate) * up) @ w_down


def hgrn2_mixer(q, f_logit, i, g, lb, gnorm):
    B, S, _ = q.shape
    n_chunks = S // CHUNK
    lbf = lb.astype(jnp.float32)
    f = lbf + (1.0 - lbf) * jax.nn.sigmoid(f_logit.astype(jnp.float32))
    k = 1.0 - f
    logf = jnp.log(f)

    def to_chunks(t):
        return t.astype(jnp.float32).reshape(B, n_chunks, CHUNK, HG_HEADS, HG_HEAD_DIM).transpose(1, 0, 3, 2, 4)

    qc, kc, vc = to_chunks(q), to_chunks(k), to_chunks(i)
    bc = jnp.cumsum(to_chunks(logf), axis=3)
    causal = jnp.tril(jnp.ones((CHUNK, CHUNK), dtype=bool))[None, None, :, :, None]

    def step(state, inp):
        qt, kt, vt, bt = inp
        diff = bt[:, :, :, None, :] - bt[:, :, None, :, :]
        decay = jnp.exp(jnp.where(causal, diff, -jnp.inf))
        a = jnp.einsum('bhtd,bhsd,bhtsd->bhts', qt, kt, decay)
        o_intra = jnp.einsum('bhts,bhsv->bhtv', a, vt)
        o_inter = jnp.einsum('bhtd,bhdv->bhtv', qt * jnp.exp(bt), state)
        b_last = bt[:, :, -1:, :]
        k_dec = kt * jnp.exp(b_last - bt)
        new_state = jnp.exp(b_last[:, :, 0, :])[..., None] * state + jnp.einsum('bhsd,bhsv->bhdv', k_dec, vt)
        return new_state, o_intra + o_inter

    s0 = jnp.zeros((B, HG_HEADS, HG_HEAD_DIM, HG_HEAD_DIM), jnp.float32)
    _, o = lax.scan(step, s0, (qc, kc, vc, bc))
    o = o.transpose(1, 0, 3, 2, 4).reshape(B, S, HG_HEADS, HG_HEAD_DIM)
    o = head_rmsnorm(o, gnorm).reshape(B, S, HG_WIDTH)
    return o * jax.nn.silu(g.astype(jnp.float32))


def stick_breaking_mixer(q, k, v, gnorm):
    B, S, _ = q.shape
    n_blk = S // SB_BLOCK
    scale = SB_HEAD_DIM ** -0.5

    def heads(t):
        return t.astype(jnp.float32).reshape(B, S, SB_HEADS, SB_HEAD_DIM).transpose(0, 2, 1, 3)

    qh, kh, vh = heads(q), heads(k), heads(v)
    qb = qh.reshape(B, SB_HEADS, n_blk, SB_BLOCK, SB_HEAD_DIM).transpose(2, 0, 1, 3, 4)
    key_pos = jnp.arange(S)

    def block(args):
        q_blk, blk = args
        z = jnp.einsum('bhtd,bhsd->bhts', q_blk, kh) * scale
        q_pos = blk * SB_BLOCK + jnp.arange(SB_BLOCK)
        before = (key_pos[None, :] < q_pos[:, None])[None, None]
        log_beta = jax.nn.log_sigmoid(z)
        log_one_minus = jnp.where(before, jax.nn.log_sigmoid(-z), 0.0)
        between = lax.cumsum(log_one_minus, axis=3, reverse=True) - log_one_minus
        weights = jnp.exp(jnp.where(before, log_beta + between, -jnp.inf))
        return jnp.einsum('bhts,bhsd->bhtd', weights, vh)

    o = lax.map(block, (qb, jnp.arange(n_blk)))
    o = o.transpose(1, 0, 3, 2, 4).reshape(B, S, SB_HEADS, SB_HEAD_DIM)
    return head_rmsnorm(o, gnorm).reshape(B, S, SB_WIDTH)


def memory_cross_attention(h, m, w_q, w_kv, w_o):
    B, S, _ = h.shape
    q = (h @ w_q).reshape(B, S, MEM_HEADS, MEM_HEAD_DIM)
    k, v = jnp.split(m @ w_kv, 2, axis=-1)
    k = k.reshape(B, N_MEM, MEM_HEADS, MEM_HEAD_DIM)
    v = v.reshape(B, N_MEM, MEM_HEADS, MEM_HEAD_DIM)
    s = jnp.einsum('bthd,bnhd->bhtn', q, k).astype(jnp.float32) * (MEM_HEAD_DIM ** -0.5)
    p = jax.nn.softmax(s, axis=-1).astype(h.dtype)
    o = jnp.einsum('bhtn,bnhd->bthd', p, v).reshape(B, S, D_MODEL)
    return o @ w_o


def setup_inputs(seed: int = 0) -> dict:
    key = jax.random.key(seed)
    ks = jax.random.split(key, 24)
    f32 = jnp.float32

    def w(k, shape, fan_in, gain=1.0):
        return (jax.random.normal(k, shape, f32) * (gain * fan_in ** -0.5)).astype(f32)

    def norm_gain(k, shape):
        return (1.0 + 0.02 * jax.random.normal(k, shape, f32)).astype(f32)

    return {
        "x": jax.random.normal(ks[0], (BATCH, SEQ, D_MODEL), f32),
        "mem": jax.random.normal(ks[1], (BATCH, N_MEM, D_MODEL), f32),
        "ffn1_norm": norm_gain(ks[2], (DEPTH, D_MODEL)),
        "ffn1_w_gu": w(ks[3], (DEPTH, D_MODEL, 2 * D_FF), D_MODEL),
        "ffn1_w_down": w(ks[4], (DEPTH, D_FF, D_MODEL), D_FF),
        "mix_norm": norm_gain(ks[5], (DEPTH, D_MODEL)),
        "w_in": w(ks[6], (DEPTH, D_MODEL, IN_COLS), D_MODEL),
        "hg_lb_raw": (0.5 * jax.random.normal(ks[7], (DEPTH + 1, HG_WIDTH), f32)).astype(f32),
        "hg_gnorm": norm_gain(ks[8], (DEPTH, HG_WIDTH)),
        "sb_gnorm": norm_gain(ks[9], (DEPTH, SB_WIDTH)),
        "w_out": w(ks[10], (DEPTH, MIX_WIDTH, D_MODEL), MIX_WIDTH),
        "mem_q_norm": norm_gain(ks[11], (DEPTH, D_MODEL)),
        "mem_kv_norm": norm_gain(ks[12], (DEPTH, D_MODEL)),
        "mem_w_q": w(ks[13], (DEPTH, D_MODEL, D_MODEL), D_MODEL),
        "mem_w_kv": w(ks[14], (DEPTH, D_MODEL, 2 * D_MODEL), D_MODEL),
        "mem_w_o": w(ks[15], (DEPTH, D_MODEL, D_MODEL), D_MODEL),
        "ffn2_norm": norm_gain(ks[16], (DEPTH, D_MODEL)),
        "ffn2_w_gu": w(ks[17], (DEPTH, D_MODEL, 2 * D_FF), D_MODEL),
        "ffn2_w_down": w(ks[18], (DEPTH, D_FF, D_MODEL), D_FF),
        "final_norm": norm_gain(ks[19], (D_MODEL,)),
    }


def reference(x, mem, ffn1_norm, ffn1_w_gu, ffn1_w_down, mix_norm, w_in, hg_lb_raw,
              hg_gnorm, sb_gnorm, w_out, mem_q_norm, mem_kv_norm, mem_w_q, mem_w_kv,
              mem_w_o, ffn2_norm, ffn2_w_gu, ffn2_w_down, final_norm):
    lb_all = jnp.cumsum(jax.nn.softmax(hg_lb_raw.astype(jnp.float32), axis=0), axis=0)
    for l in range(DEPTH):
        x = x + 0.5 * swiglu(rmsnorm(x, ffn1_norm[l]), ffn1_w_gu[l], ffn1_w_down[l])

        h = rmsnorm(x, mix_norm[l])
        proj = h @ w_in[l]
        hg_q, hg_f, hg_i, hg_g, sb_q, sb_k, sb_v = jnp.split(
            proj, np.cumsum([HG_WIDTH] * 4 + [SB_WIDTH] * 2).tolist(), axis=-1)
        o_hg = hgrn2_mixer(hg_q, hg_f, hg_i, hg_g, lb_all[l], hg_gnorm[l])
        o_sb = stick_breaking_mixer(sb_q, sb_k, sb_v, sb_gnorm[l])
        mixed = jnp.concatenate([o_hg, o_sb], axis=-1).astype(x.dtype)
        x = x + mixed @ w_out[l]

        x = x + memory_cross_attention(rmsnorm(x, mem_q_norm[l]), rmsnorm(mem, mem_kv_norm[l]),
                                       mem_w_q[l], mem_w_kv[l], mem_w_o[l])

        x = x + 0.5 * swiglu(rmsnorm(x, ffn2_norm[l]), ffn2_w_gu[l], ffn2_w_down[l])
    return rmsnorm(x, final_norm)
```

```python
import contextlib
import numpy as np
import ml_dtypes
import concourse.bass as bass
import concourse.mybir as mybir
from concourse.bass_utils import run_bass_kernel_spmd

F32 = mybir.dt.float32
BF16 = mybir.dt.bfloat16
AF = mybir.ActivationFunctionType
ALU = mybir.AluOpType
AX = mybir.AxisListType

D = 1024
DFF = 2816
NHC = 22
NMEM = 256
EPS = 1e-6
CH = 512
PKW = 8192

ENGS = ["pe", "act", "dve", "pool", "sp"]
SAME_ENGINE_SYNC = True


class Op:
    __slots__ = ("eng", "fn", "kind", "deps", "signal", "sig_n", "dma_i", "cc_i", "idx")


class Prog:
    NS = 8
    EP = 8000

    def __init__(self, nc):
        self.nc = nc
        self.q = {e: [] for e in ENGS}
        self.lastw = {}
        self.rd = {}
        self.ndma = {"sp": 0, "pool": 0, "act": 0}
        self.ncc = 0
        self.all_dma = {"sp": [], "pool": [], "act": []}
        self.all_cc = []

    def op(self, eng, fn, r=(), w=(), kind="c"):
        o = Op()
        o.eng = eng; o.fn = fn; o.kind = kind; o.deps = set(); o.signal = False
        o.sig_n = 0; o.dma_i = -1; o.cc_i = -1
        for k in r:
            d = self.lastw.get(k)
            if d is not None:
                o.deps.add(d)
        for k in w:
            d = self.lastw.get(k)
            if d is not None:
                o.deps.add(d)
            for x in self.rd.get(k, {}).values():
                o.deps.add(x)
        o.deps.discard(o)
        for k in w:
            self.lastw[k] = o
            self.rd[k] = {}
        wset = set(w)
        for k in r:
            if k not in wset:
                self.rd.setdefault(k, {})[eng] = o
        if kind == "dma":
            o.dma_i = self.ndma[eng]; self.ndma[eng] += 1
            self.all_dma[eng].append(o)
        elif kind == "cc":
            o.cc_i = self.ncc; self.ncc += 1
            self.all_cc.append(o)
        o.idx = len(self.q[eng])
        self.q[eng].append(o)
        return o

    def barrier(self):
        last = []
        for e in ENGS:
            for o in reversed(self.q[e]):
                if o.kind == "c" and o.fn is not None:
                    last.append(o); break
        for e in self.all_dma:
            last.extend(self.all_dma[e][-self.NS:])
        last.extend(self.all_cc)
        for e in ENGS:
            o = Op()
            o.eng = e; o.fn = None; o.kind = "c"; o.deps = set(last); o.signal = False
            o.sig_n = 0; o.dma_i = -1; o.cc_i = -1; o.idx = len(self.q[e])
            self.q[e].append(o)
        self.lastw = {}
        self.rd = {}

    def emit(self):
        nc = self.nc
        for e in ENGS:
            for o in self.q[e]:
                for d in o.deps:
                    if d.kind == "c":
                        if d.eng == e and (e == "pe" or not SAME_ENGINE_SYNC):
                            continue
                        d.signal = True
        nsig = {}
        for e in ENGS:
            n = 0
            for o in self.q[e]:
                if o.kind == "c" and o.signal:
                    n += 1; o.sig_n = n
            nsig[e] = n
        csem = {e: [nc.alloc_semaphore(f"c_{e}_{k}") for k in range(nsig[e] // self.EP + 1)] for e in ENGS}
        dsem = {e: [nc.alloc_semaphore(f"d_{e}_{k}") for k in range(self.NS)] for e in self.all_dma}
        ccsem = [nc.alloc_semaphore(f"cc_{k}") for k in range(self.ncc)]
        endsem = nc.alloc_semaphore("endsem")
        allsems = [x for e in ENGS for x in csem[e]] + [x for e in dsem for x in dsem[e]] + ccsem
        NS, EP = self.NS, self.EP

        def target(d):
            if d.kind == "c":
                n = d.sig_n - 1
                return csem[d.eng][n // EP], n % EP + 1
            if d.kind == "dma":
                return dsem[d.eng][d.dma_i % NS], 16 * (d.dma_i // NS + 1)
            return ccsem[d.cc_i], 1

        def run(e, engine):
            waited = {}

            def wait(sem, val):
                key = id(sem)
                if waited.get(key, 0) >= val:
                    return
                waited[key] = val
                engine.wait_ge(sem, val)

            for o in self.q[e]:
                need = {}
                for d in o.deps:
                    if d.kind == "c" and d.eng == e and (e == "pe" or not SAME_ENGINE_SYNC):
                        continue
                    s, v = target(d)
                    k = id(s)
                    if k not in need or need[k][1] < v:
                        need[k] = (s, v)
                if o.kind == "dma" and o.dma_i >= NS:
                    s = dsem[e][o.dma_i % NS]; v = 16 * (o.dma_i // NS)
                    k = id(s)
                    if k not in need or need[k][1] < v:
                        need[k] = (s, v)
                for s, v in need.values():
                    wait(s, v)
                if o.fn is None:
                    continue
                ins = o.fn(engine)
                if o.kind == "c":
                    if o.signal:
                        n = o.sig_n - 1
                        ins.then_inc(csem[e][n // EP], 1)
                elif o.kind == "dma":
                    ins.then_inc(dsem[e][o.dma_i % NS], 16)
                else:
                    ins.then_inc(ccsem[o.cc_i], 1)

        with nc.Block() as block:
            @block.tensor
            def _(t):
                run("pe", t)

            @block.scalar
            def _(s):
                run("act", s)

            @block.vector
            def _(v):
                run("dve", v)

            @block.gpsimd
            def _(g):
                run("pool", g)

            @block.sync
            def _(s):
                run("sp", s)


class Arena:
    def __init__(self, ap, size):
        self.ap = ap; self.size = size; self.off = 0

    def reset(self):
        self.off = 0

    def alloc(self, shape, dt):
        n = int(np.prod(shape[1:]))
        nb = n * (4 if dt == F32 else 2)
        nb = (nb + 63) // 64 * 64
        ne = nb // 2
        assert self.off + ne <= self.size, ("arena overflow", self.off, ne, self.size)
        v = self.ap[0:shape[0], self.off:self.off + ne]
        self.off += ne
        if dt == F32:
            v = v.bitcast(F32)[:, 0:n]
        else:
            v = v[:, 0:n]
        if len(shape) == 3:
            v = v.rearrange("p (a b) -> p a b", a=shape[1])
        elif len(shape) == 4:
            v = v.rearrange("p (a b c) -> p a b c", a=shape[1], b=shape[2])
        return v


def build(S, debug=False, stop=None):
    T = S // 2
    NCH = T // CH
    NT = S // 128
    nc = bass.Bass("TRN2", target_bir_lowering=False)
    P = Prog(nc)

    def din(name, shape, dt=F32):
        return nc.dram_tensor(name, shape, dt, kind="ExternalInput").ap()

    x_d = din("x", [T, D]); mem_d = din("mem", [NMEM, D])
    wgu_d = [din("w_gu1", [D, 2 * DFF]), din("w_gu2", [D, 2 * DFF])]
    wd_d = [din("w_d1", [DFF, D]), din("w_d2", [DFF, D])]
    win_d = din("w_in", [D, 3584]); wout_d = din("w_out", [D, D])
    wq_d = din("w_q", [D, D]); wkv_d = din("w_kv", [D, 2 * D]); wo_d = din("w_o", [D, D])
    gains_d = din("gains", [6, D])
    hgp_d = din("hgp", [3, 256])
    sbg_d = din("sbg", [64, 4])
    cb_d = din("cb", [128, 776], BF16)
    cf_d = din("cf", [128, 128 + 4])
    out_d = nc.dram_tensor("out", [T, D], F32, kind="ExternalOutput").ap()

    dbg_list = []

    def dscr(name, shape, dt=BF16):
        a = nc.dram_tensor(name, shape, dt, kind="Internal").ap()
        if debug and name in debug:
            dbg_list.append((a, nc.dram_tensor("dbg_" + name, shape, dt, kind="ExternalOutput").ap()))
        return a

    def finish():
        P.barrier()
        for a, b in dbg_list:
            P.op("sp", lambda e, a=a, b=b: e.dma_start(out=b, in_=a), kind="dma")
        P.barrier()
        P.emit()
        return nc

    wgu_s = [dscr(f"wgu_s{i}", [NHC, 128, 8, 2, 128]) for i in range(2)]
    wd_s = [dscr(f"wd_s{i}", [NHC, 128, D]) for i in range(2)]
    win_s = dscr("win_s", [7, 128, 8, 512])
    wout_s = dscr("wout_s", [2, 128, 8, 512])
    wq_s = dscr("wq_s", [2, 128, 8, 512])
    wkv_s = dscr("wkv_s", [4, 128, 8, 512])
    wo_s = dscr("wo_s", [2, 128, 8, 512])
    x1_s = dscr("x1_s", [T, D], F32)
    big = dscr("big", [NCH, 3 * 128, PKW])
    ain = dscr("ain", [NCH, 128, PKW])
    seq = dscr("seq", [2, NCH, 128, PKW])
    msb_my = dscr("msb_my", [512, T]); mhg_my = dscr("mhg_my", [2, T, 256])
    mhg_in = dscr("mhg_in", [2, T, 256]); mhg_out = dscr("mhg_out", [2, 2 * T, 256])
    msb_in = dscr("msb_in", [2, 256, T]); msb_out = dscr("msb_out", [2, 512, T])
    RG = [[0, 1], [2, 3], [4, 5], [6, 7]]

    es = contextlib.ExitStack()
    with es:
        ARENA_E = 94 * 1024
        arena_t = es.enter_context(nc.sbuf_tensor("arena", [128, ARENA_E], BF16))
        A = Arena(arena_t, ARENA_E)
        cb = es.enter_context(nc.sbuf_tensor("cb_sb", [128, 776], BF16))
        cf = es.enter_context(nc.sbuf_tensor("cf_sb", [128, 132], F32))
        ident = cb[:, 0:128]; tri = cb[:, 128:256]; omt = cb[:, 256:384]
        sbmask = cb[:, 384:512]; causal = cb[:, 512:640]
        Mm = cb[:, 640:768]; ind3 = cb[:, 768:772]
        ones_bf = es.enter_context(nc.sbuf_tensor("ones_bf", [128, 128], BF16))
        ps = [es.enter_context(nc.psum_tensor(f"ps{i}", [128, 512], F32)) for i in range(8)]

        def dma(eng, out, in_, r=(), w=(), **kw):
            return P.op(eng, lambda e, out=out, in_=in_, kw=kw: e.dma_start(out=out, in_=in_, **kw), r=r, w=w, kind="dma")

        def mm(out, lhsT, rhs, start, stop, r=(), w=()):
            return P.op("pe", lambda e: e.matmul(out, lhsT, rhs, start=start, stop=stop), r=r, w=w)

        def tp(out, in_, r=(), w=()):
            return P.op("pe", lambda e: e.transpose(out, in_, ident), r=list(r) + ["cb"], w=w)

        def act(out, in_, func, r=(), w=(), **kw):
            return P.op("act", lambda e: e.activation(out, in_, func, **kw), r=r, w=w)

        def vop(eng, name, *args, r=(), w=(), **kw):
            return P.op(eng, lambda e: getattr(e, name)(*args, **kw), r=r, w=w)

        dma("sp", cb[:], cb_d, w=["cb"])
        dma("sp", cf[:], cf_d, w=["cf"])
        vop("pool", "memset", ones_bf[:], 1.0, w=["ones"])

        def prep_w(src, dst, ngrp, gw, name):
            for kc in range(8):
                dma("pool", dst[:, :, kc, :], src[kc * 128:(kc + 1) * 128, :].rearrange("p (g c) -> g p c", c=gw),
                    w=[(name, kc)])

        def prep_ffn(i):
            for kc in range(8):
                for gu in range(2):
                    dma("pool", wgu_s[i][:, :, kc, gu, :],
                        wgu_d[i][kc * 128:(kc + 1) * 128, gu * DFF:(gu + 1) * DFF].rearrange("p (h c) -> h p c", c=128),
                        w=[(f"wgu{i}", kc, gu)])
            for q4 in range(2):
                dma("pool", wd_s[i][q4 * 11:(q4 + 1) * 11],
                    wd_d[i][q4 * 11 * 128:(q4 + 1) * 11 * 128, :].rearrange("(h p) c -> h p c", p=128), w=[(f"wd{i}", q4)])

        def wgu_keys(i):
            return [(f"wgu{i}", kc, gu) for kc in range(8) for gu in range(2)]

        def wd_keys(i):
            return [(f"wd{i}", q4) for q4 in range(2)]

        def wkeys(name):
            return [(name, kc) for kc in range(8)]

        prep_ffn(0)
        prep_w(win_d, win_s, 7, 512, "win")
        prep_w(wout_d, wout_s, 2, 512, "wout")
        prep_w(wq_d, wq_s, 2, 512, "wq")
        prep_w(wkv_d, wkv_s, 4, 512, "wkv")
        prep_w(wo_d, wo_s, 2, 512, "wo")
        prep_ffn(1)

        if stop == "p0":
            return finish()
        def alloc_common():
            t = {}
            t["xres"] = A.alloc([128, 4, D], F32)
            t["xn"] = A.alloc([128, 4, D], BF16)
            t["junk"] = A.alloc([128, D], BF16)
            t["xnT"] = A.alloc([128, 8, CH], BF16)
            t["hT"] = A.alloc([128, NHC, CH], BF16)
            t["wgu"] = [A.alloc([128, 8, 2, 128], BF16) for _ in range(3)]
            t["wd"] = A.alloc([128, NHC, D], BF16)
            t["wg"] = [A.alloc([128, 8, 512], BF16) for _ in range(2)]
            t["gain"] = [A.alloc([128, D], F32) for _ in range(3)]
            t["ss"] = A.alloc([128, 8], F32)
            t["rstd"] = A.alloc([128, 8], F32)
            t["mhalf"] = A.alloc([128, 8], F32)
            t["sg"] = [A.alloc([128, CH], BF16) for _ in range(2)]
            return t

        state = {"wgu_i": 0, "wg_i": 0}

        def load_gain(t, slot, row):
            dma("sp", t["gain"][slot], gains_d[row:row + 1, :].to_broadcast([128, D]), w=[("gain", slot)])

        def rms_T(t, gslot, src="xres", ntt=4, dstT="xnT"):
            xres = t[src]
            for tt in range(ntt):
                act(t["junk"], xres[:, tt, :], AF.Square, r=[(src, tt)], w=["junk", ("ss", tt)], accum_out=t["ss"][:, tt:tt + 1])
            vop("pool", "tensor_scalar", t["rstd"][:, 0:ntt], t["ss"][:, 0:ntt], 1.0 / D, EPS, ALU.mult, ALU.add,
                r=[("ss", tt) for tt in range(ntt)], w=["rstd_a"])
            vop("pool", "tensor_tensor", t["rstd"][:, 0:ntt], t["rstd"][:, 0:ntt], t["mhalf"][:, 0:ntt], ALU.pow,
                r=["rstd_a", "mhalf"], w=["rstd"])
            for tt in range(ntt):
                vop("dve", "scalar_tensor_tensor", t["xn"][:, tt, :], xres[:, tt, :], t["rstd"][:, tt:tt + 1],
                    t["gain"][gslot], ALU.mult, ALU.mult, r=[(src, tt), "rstd", ("gain", gslot)], w=[("xn", tt)])
            for tt in range(ntt):
                for half in range(2):
                    pst = ps[6 + half][:, 0:256].bitcast(BF16).rearrange("p (a b) -> p a b", a=4)
                    for j in range(4):
                        kc = half * 4 + j
                        tp(pst[:, j, :], t["xn"][:, tt, kc * 128:(kc + 1) * 128], r=[("xn", tt)], w=[("ps", 6 + half)])
                    eng = "act" if half == 0 else "dve"
                    if eng == "act":
                        act(t[dstT][:, half * 4:half * 4 + 4, tt * 128:(tt + 1) * 128], pst, AF.Copy,
                            r=[("ps", 6 + half)], w=[(dstT, tt)])
                    else:
                        vop("dve", "tensor_copy", t[dstT][:, half * 4:half * 4 + 4, tt * 128:(tt + 1) * 128], pst,
                            r=[("ps", 6 + half)], w=[(dstT, tt)])

        def ffn(t, i, ci):
            for q4 in range(2):
                dma("sp", t["wd"][:, q4 * 11:(q4 + 1) * 11, :], wd_s[i][q4 * 11:(q4 + 1) * 11].rearrange("h p c -> p h c"),
                    r=wd_keys(i), w=[("wdt", q4)])
            xT = [("xnT", tt) for tt in range(4)]
            for hc in range(NHC):
                wi = state["wgu_i"] % 3; state["wgu_i"] += 1
                wt = t["wgu"][wi]
                dma("sp", wt, wgu_s[i][hc], r=wgu_keys(i), w=[("wgut", wi)])
                pg = ps[(hc % 2) * 2]; pu = ps[(hc % 2) * 2 + 1]
                for gu, pp in ((0, pg), (1, pu)):
                    for kc in range(8):
                        mm(pp[:], wt[:, kc, gu, :], t["xnT"][:, kc, :], kc == 0, kc == 7,
                           r=[("wgut", wi)] + xT, w=[("ps", (hc % 2) * 2 + gu)])
                sg = t["sg"][hc % 2]
                act(sg, pg[:], AF.Silu, r=[("ps", (hc % 2) * 2)], w=[("sg", hc % 2)])
                vop("dve", "tensor_tensor", t["hT"][:, hc, :], sg, pu[:], ALU.mult,
                    r=[("sg", hc % 2), ("ps", (hc % 2) * 2 + 1)], w=[("hT", hc)])
            hk = [("hT", hc) for hc in range(NHC)]
            n = 0
            for tt in range(4):
                for half in range(2):
                    bk = 4 + n % 2; pp = ps[bk]; n += 1
                    for hc in range(NHC):
                        mm(pp[:], t["hT"][:, hc, tt * 128:(tt + 1) * 128], t["wd"][:, hc, half * 512:(half + 1) * 512],
                           hc == 0, hc == NHC - 1, r=[("hT", hc), ("wdt", hc // 11)], w=[("ps", bk)])
                    vop("dve", "scalar_tensor_tensor", t["xres"][:, tt, half * 512:(half + 1) * 512], pp[:], 0.5,
                        t["xres"][:, tt, half * 512:(half + 1) * 512], ALU.mult, ALU.add,
                        r=[("ps", bk), ("xres", tt)], w=[("xres", tt)])

        def load_wg(t, src, g, keys):
            wi = state["wg_i"] % 2; state["wg_i"] += 1
            dma("sp", t["wg"][wi], src[g], r=keys, w=[("wgt", wi)])
            return wi

        A.reset()
        t = alloc_common()
        pk = [A.alloc([128, PKW], BF16) for _ in range(2)]
        vop("pool", "memset", t["mhalf"], -0.5, w=["mhalf"])
        load_gain(t, 0, 0); load_gain(t, 1, 1)
        par_holder = {}

        def sp_par(e):
            if "v" not in par_holder:
                par_holder["v"] = e.partition_id() % 2
            return par_holder["v"]

        for ci in range(NCH):
            for tt in range(4):
                dma("sp", t["xres"][:, tt, :], x_d[ci * CH + tt * 128: ci * CH + (tt + 1) * 128, :], w=[("xres", tt)])
            if stop == "p1x":
                return finish()
            rms_T(t, 0)
            if stop == "p1a":
                return finish()
            ffn(t, 0, ci)
            if stop == "p1b":
                return finish()
            for tt in range(4):
                dma("sp", x1_s[ci * CH + tt * 128: ci * CH + (tt + 1) * 128, :], t["xres"][:, tt, :], r=[("xres", tt)],
                    w=[("x1s", ci, tt)])
            if stop == "p1b1":
                return finish()
            rms_T(t, 1)
            if stop == "p1b2":
                return finish()
            xT = [("xnT", tt) for tt in range(4)]
            for g in range(2):
                wi = load_wg(t, win_s, g, wkeys("win"))
                for i4 in range(4):
                    pp = ps[4 + i4 % 2]
                    for kc in range(8):
                        mm(pp[:], t["wg"][wi][:, kc, i4 * 128:(i4 + 1) * 128], t["xnT"][:, kc, :], kc == 0, kc == 7,
                           r=[("wgt", wi)] + xT, w=[("ps", 4 + i4 % 2)])
                    if i4 % 2 == 0:
                        act(pk[g][:, i4 * 512:(i4 + 1) * 512], pp[:], AF.Copy, r=[("ps", 4 + i4 % 2)], w=[("pk", g)])
                    else:
                        vop("dve", "tensor_copy", pk[g][:, i4 * 512:(i4 + 1) * 512], pp[:], r=[("ps", 4 + i4 % 2)], w=[("pk", g)])
            if stop == "p1b3":
                return finish()
            for g in range(2, 7):
                if stop == f"p1g{g}":
                    return finish()
                wi = load_wg(t, win_s, g, wkeys("win"))
                for tt in range(4):
                    pp = ps[4 + tt % 2]
                    for kc in range(8):
                        mm(pp[:], t["xnT"][:, kc, tt * 128:(tt + 1) * 128], t["wg"][wi][:, kc, :], kc == 0, kc == 7,
                           r=[("wgt", wi), ("xnT", tt)], w=[("ps", 4 + tt % 2)])
                    if g == 4:
                        for m in range(2):
                            dst = pk[m][:, 6144 + tt * 512: 6144 + (tt + 1) * 512].bitcast(F32)
                            if tt % 2 == 0:
                                act(dst, pp[:, m * 256:(m + 1) * 256], AF.Copy, r=[("ps", 4 + tt % 2)], w=[("pk", m)])
                            else:
                                vop("dve", "tensor_copy", dst, pp[:, m * 256:(m + 1) * 256], r=[("ps", 4 + tt % 2)], w=[("pk", m)])
                    else:
                        m = 0 if g < 4 else 1
                        base = 2048 if g in (2, 5) else 4096
                        dstv = pk[m][:, base:base + 2048].rearrange("p (s a c) -> p s a c", s=2, a=4)[:, :, tt, :]
                        src = pp[:].rearrange("p (s c) -> p s c", s=2)
                        if tt % 2 == 0:
                            act(dstv, src, AF.Copy, r=[("ps", 4 + tt % 2)], w=[("pk", m)])
                        else:
                            vop("dve", "tensor_copy", dstv, src, r=[("ps", 4 + tt % 2)], w=[("pk", m)])
            if stop == "p1c":
                return finish()
            dma("sp", big[ci, 256:384, :], pk[0], r=[("pk", 0)], w=[("big", ci, 2)])
            dma("sp", ain[ci], pk[1], r=[("pk", 1)], w=[("ain", ci)])
            if stop == "p1d":
                return finish()
            P.op("pool", lambda e, ci=ci: e.collective_compute("AllGather", ALU.bypass, replica_groups=RG,
                                                               ins=[ain[ci]], outs=[big[ci, 0:256, :]]),
                 r=[("ain", ci)], w=[("big", ci, 0), ("big", ci, 1)], kind="cc")

        P.barrier()
        if stop == "p1":
            return finish()
        if debug:
            early = nc.dram_tensor("dbg_x1_early", [T, D], F32, kind="ExternalOutput").ap()
            P.op("sp", lambda e: e.dma_start(out=early, in_=x1_s), kind="dma")
            P.barrier()

        A.reset()
        QK = A.alloc([128, 4, S], BF16)
        V = A.alloc([128, NT, 256], BF16)
        hq = [A.alloc([128, 4, 256], BF16) for _ in range(2)]
        hi = [A.alloc([128, 4, 256], BF16) for _ in range(2)]
        hg = [A.alloc([128, 4, 256], BF16) for _ in range(2)]
        hfr = [A.alloc([128, 2048], BF16) for _ in range(2)]
        hf = [x.bitcast(F32).rearrange("p (a c) -> p a c", a=4) for x in hfr]
        lbr = A.alloc([128, 3, 256], F32)
        lb = A.alloc([128, 256], F32); oml = A.alloc([128, 256], F32); gnb = A.alloc([128, 256], F32)
        sbg = A.alloc([64, 4], F32)
        Sst = [A.alloc([128, 128], F32) for _ in range(2)]
        Stmp = [A.alloc([128, 128], F32) for _ in range(2)]
        Sbf = [A.alloc([128, 128], BF16) for _ in range(2)]

        def W2(n, shape=(128, 256), dt=F32):
            return [A.alloc(list(shape), dt) for _ in range(n)]
        lhi = W2(2, dt=BF16); llo = W2(2, dt=BF16); e1 = W2(2); f_ = W2(2); logf = W2(2); kk = W2(2); eq = W2(2); ek = W2(2); e2 = W2(2); sgg = W2(2)
        qh = W2(2, dt=BF16); kh = W2(2, dt=BF16)
        qhT = W2(2, (128, 2, 128), BF16); khT0 = W2(2, (128, 2, 128), BF16); khT1 = W2(2, (128, 2, 128), BF16)
        am = W2(2, (128, 2, 128), BF16)
        esc = W2(2, (128, 2, 3), F32)
        hss = W2(2, (128, 2), F32); hln = W2(2, (128, 2), F32); hrs = W2(2, (128, 2), F32)
        mst = W2(2, (128, 256), BF16)
        junk2 = A.alloc([128, 128], BF16)
        E_ = W2(2, (128, 512), F32); L_ = W2(2, (128, 512), BF16); X_ = W2(2, (128, 512), F32); Wt = W2(2, (128, 512), BF16)
        sq_ = W2(2, (64, 512), BF16); rl_ = W2(2, (64, 512), F32); rs_ = W2(2, (64, 512), F32); mo_ = W2(2, (64, 512), BF16)

        if stop == "p2a0":
            return finish()
        if stop == "p2a":
            return finish()

        big4 = big.rearrange("n (s p) c -> n s p c", s=3)
        for hfi in range(2):
            def fnq(e, hfi=hfi):
                par = sp_par(e)
                idx = ((1 - par) * 2) if hfi == 0 else (par + 1)
                return e.dma_start(out=seq[hfi], in_=big4[:, bass.ds(idx, 1), :, :].rearrange("n o p c -> (n o) p c"))
            P.op("sp", fnq, w=[("seq", hfi)], kind="dma")

        if stop == "p2b":
            return finish()

        def slot_ap(ci, hfi, c0, c1):
            return seq[hfi, ci][:, c0:c1]

        def dyn_dma(out, src, r=(), w=(), rearr=None, hfi=0):
            if rearr is not None:
                src = rearr(src)
            return dma("sp", out, src, r=list(r) + [("seq", hfi)], w=w)

        dma("sp", lbr.rearrange("p a c -> p (a c)"), hgp_d.rearrange("a c -> (a c)").rearrange("(o n) -> o n", o=1).to_broadcast([128, 768]), w=["lbr"])
        dma("sp", sbg, sbg_d, w=["sbg"])
        vop("dve", "tensor_tensor", lb, lbr[:, 1, :], lbr[:, 0, :], ALU.subtract, r=["lbr"], w=["lb0"])
        act(lb, lb, AF.Exp, r=["lb0"], w=["lb1"])
        vop("dve", "tensor_scalar", lb, lb, 1.0, None, ALU.add, r=["lb1"], w=["lb2"])
        vop("dve", "reciprocal", lb, lb, r=["lb2"], w=["lb"])
        vop("dve", "tensor_scalar", oml, lb, -1.0, 1.0, ALU.mult, ALU.add, r=["lb"], w=["oml"])
        vop("dve", "tensor_copy", gnb, lbr[:, 2, :], r=["lbr"], w=["gnb"])
        for h in range(2):
            vop("dve", "memset", Sst[h], 0.0, w=[("S", h)])
            for i in range(2):
                vop("pool", "memset", khT0[i][:, h, :], 0.0, w=[("khT0", i, h)])
                vop("pool", "memset", khT1[i][:, h, :], 0.0, w=[("khT1", i, h)])

        if stop == "p2c":
            return finish()
        for hfi in range(2):
            for ci in range(NCH):
                n0 = hfi * T + ci * CH
                deps = [("big", ci, 0), ("big", ci, 1), ("big", ci, 2)]
                dyn_dma(QK[:, :, n0:n0 + CH], slot_ap(ci, hfi, 0, 2048), w=[("QK", n0 // CH)],
                        rearr=lambda s: s.rearrange("p (i t) -> p i t", i=4), hfi=hfi)
                dyn_dma(V[:, n0 // 128:n0 // 128 + 4, :], slot_ap(ci, hfi, 2048, 3072), w=[("V", n0 // CH)],
                        rearr=lambda s: s.rearrange("p (a c) -> p a c", a=4), hfi=hfi)

        if stop == "p2l":
            return finish()
        mhg_v = mhg_in.rearrange("h (n p) c -> (h n) p c", p=128)
        for hfi in range(2):
            for ci in range(NCH):
                bi = (hfi * NCH + ci) % 2
                dyn_dma(hq[bi], slot_ap(ci, hfi, 3072, 4096), w=[("hq", bi)], rearr=lambda s: s.rearrange("p (a c) -> p a c", a=4), hfi=hfi)
                dyn_dma(hi[bi], slot_ap(ci, hfi, 4096, 5120), w=[("hi", bi)], rearr=lambda s: s.rearrange("p (a c) -> p a c", a=4), hfi=hfi)
                dyn_dma(hg[bi], slot_ap(ci, hfi, 5120, 6144), w=[("hg", bi)], rearr=lambda s: s.rearrange("p (a c) -> p a c", a=4), hfi=hfi)
                dyn_dma(hfr[bi], slot_ap(ci, hfi, 6144, 8192), w=[("hf", bi)], hfi=hfi)
                if stop == "hgA0":
                    return finish()
                for tt in range(4):
                    gt = (hfi * T + ci * CH) // 128 + tt
                    w_ = gt % 2
                    K = lambda name: (name, w_)
                    act(e1[w_], hf[bi][:, tt, :], AF.Exp, r=[("hf", bi)], w=[K("e1")], scale=-1.0)
                    if stop == "hgA1":
                        return finish()
                    vop("dve", "tensor_scalar", e1[w_], e1[w_], 1.0, None, ALU.add, r=[K("e1")], w=[K("e1")])
                    if stop == "hgA2":
                        return finish()
                    vop("dve", "reciprocal", e1[w_], e1[w_], r=[K("e1")], w=[K("e1")])
                    if stop == "hgA3":
                        return finish()
                    vop("dve", "tensor_tensor", f_[w_], e1[w_], oml, ALU.mult, r=[K("e1"), "oml"], w=[K("f")])
                    vop("dve", "tensor_tensor", f_[w_], f_[w_], lb, ALU.add, r=[K("f"), "lb"], w=[K("f")])
                    act(logf[w_], f_[w_], AF.Ln, r=[K("f")], w=[K("logf")])
                    vop("dve", "tensor_scalar", kk[w_], f_[w_], -1.0, 1.0, ALU.mult, ALU.add, r=[K("f")], w=[K("kk")])
                    if stop == "hgA":
                        return finish()
                    vop("pool", "tensor_copy", lhi[w_], logf[w_], r=[K("logf")], w=[K("lhi")])
                    vop("pool", "tensor_tensor", llo[w_], logf[w_], lhi[w_], ALU.subtract, r=[K("logf"), K("lhi")], w=[K("llo")])
                    mm(ps[0][:, 0:256], Mm, lhi[w_], True, False, r=["cb", K("lhi")], w=["p_c1"])
                    mm(ps[0][:, 0:256], Mm, llo[w_], False, True, r=["cb", K("llo")], w=["p_c1"])
                    for h in range(2):
                        mm(ps[1][:, h * 4:h * 4 + 4], lhi[w_][:, h * 128:(h + 1) * 128], ind3, True, False,
                           r=["cb", K("lhi")], w=["p_sc"])
                        mm(ps[1][:, h * 4:h * 4 + 4], llo[w_][:, h * 128:(h + 1) * 128], ind3, False, True,
                           r=["cb", K("llo")], w=["p_sc"])
                    act(eq[w_], ps[0][:, 0:256], AF.Exp, r=["p_c1"], w=[K("eq")])
                    act(ek[w_], ps[0][:, 0:256], AF.Exp, r=["p_c1"], w=[K("ek")], scale=-1.0)
                    act(esc[w_], ps[1][:, 0:8].rearrange("p (h c) -> p h c", h=2)[:, :, 0:3], AF.Exp, r=["p_sc"], w=[K("esc")])
                    vop("dve", "tensor_tensor", qh[w_], hq[bi][:, tt, :], eq[w_], ALU.mult, r=[("hq", bi), K("eq")], w=[K("qh")])
                    vop("dve", "tensor_tensor", kh[w_], kk[w_], ek[w_], ALU.mult, r=[K("kk"), K("ek")], w=[K("kh")])
                    if stop == "hgB":
                        return finish()
                    act(e2[w_], hg[bi][:, tt, :], AF.Exp, r=[("hg", bi)], w=[K("e2")], scale=-1.0)
                    vop("dve", "tensor_scalar", e2[w_], e2[w_], 1.0, None, ALU.add, r=[K("e2")], w=[K("e2")])
                    vop("dve", "reciprocal", e2[w_], e2[w_], r=[K("e2")], w=[K("e2")])
                    vop("dve", "tensor_tensor", sgg[w_], e2[w_], hg[bi][:, tt, :], ALU.mult, r=[K("e2"), ("hg", bi)], w=[K("sgg")])
                    vop("dve", "tensor_tensor", sgg[w_], sgg[w_], gnb, ALU.mult, r=[K("sgg"), "gnb"], w=[K("sgg")])
                    ptq = ps[2][:, 0:128].bitcast(BF16).rearrange("p (h c) -> p h c", h=2)
                    ptk = ps[3][:, 0:128].bitcast(BF16).rearrange("p (h c) -> p h c", h=2)
                    for h in range(2):
                        tp(ptq[:, h, :], qh[w_][:, h * 128:(h + 1) * 128], r=[K("qh")], w=["ptq"])
                        tp(ptk[:, h, :], kh[w_][:, h * 128:(h + 1) * 128], r=[K("kh")], w=["ptk"])
                    act(qhT[w_], ptq, AF.Copy, r=["ptq"], w=[K("qhT")])
                    vop("dve", "tensor_copy", khT0[w_][:, :, 0:64], ptk[:, :, 0:64], r=["ptk"], w=[K("khT0")])
                    vop("dve", "tensor_copy", khT1[w_][:, :, 64:128], ptk[:, :, 64:128], r=["ptk"], w=[K("khT1")])
                    if stop == "hgC":
                        return finish()
                    for h in range(2):
                        pa = ps[4][:, h * 128:(h + 1) * 128]
                        mm(pa, khT0[w_][:, h, :], qhT[w_][:, h, :], True, False, r=[K("khT0"), K("qhT")], w=[("p_a", h)])
                        mm(pa[:, 64:128], khT1[w_][:, h, :], qhT[w_][:, h, 64:128], False, True, r=[K("khT1"), K("qhT")], w=[("p_a", h)])
                        vop("dve", "tensor_tensor", am[w_][:, h, :], pa, causal, ALU.mult, r=[("p_a", h), "cb"], w=[K("am") + (h,)])
                        act(Sbf[h], Sst[h], AF.Copy, r=[("S", h), K("esc")], w=[("Sbf", h)], scale=esc[w_][:, h, 0:1])
                        vop("dve", "tensor_scalar", Stmp[h], Sst[h], esc[w_][:, h, 1:2], None, ALU.mult,
                            r=[("S", h), K("esc")], w=[("Stmp", h)])
                        po = ps[5][:, h * 128:(h + 1) * 128]
                        ih = hi[bi][:, tt, h * 128:(h + 1) * 128]
                        mm(po, am[w_][:, h, :], ih, True, False, r=[K("am") + (h,), ("hi", bi)], w=[("p_o", h)])
                        mm(po, qhT[w_][:, h, :], Sbf[h], False, True, r=[K("qhT"), ("Sbf", h)], w=[("p_o", h)])
                        pu = ps[6][:, h * 128:(h + 1) * 128]
                        mm(pu, kh[w_][:, h * 128:(h + 1) * 128], ih, True, True, r=[K("kh"), ("hi", bi)], w=[("p_u", h)])
                        vop("dve", "scalar_tensor_tensor", Sst[h], pu, esc[w_][:, h, 2:3], Stmp[h], ALU.mult, ALU.add,
                            r=[("p_u", h), K("esc"), ("Stmp", h)], w=[("S", h)])
                        act(junk2, po, AF.Square, r=[("p_o", h)], w=["junk2", K("hss") + (h,)], accum_out=hss[w_][:, h:h + 1])
                    if stop == "hgD":
                        return finish()
                    act(hln[w_], hss[w_], AF.Ln, r=[K("hss") + (0,), K("hss") + (1,)], w=[K("hln")], scale=1.0 / 128, bias=EPS)
                    act(hrs[w_], hln[w_], AF.Exp, r=[K("hln")], w=[K("hrs")], scale=-0.5)
                    for h in range(2):
                        po = ps[5][:, h * 128:(h + 1) * 128]
                        vop("dve", "scalar_tensor_tensor", mst[w_][:, h * 128:(h + 1) * 128], po, hrs[w_][:, h:h + 1],
                            sgg[w_][:, h * 128:(h + 1) * 128], ALU.mult, ALU.mult,
                            r=[("p_o", h), K("hrs"), K("sgg")], w=[K("mst")])
                    dma("sp", mhg_v[gt], mst[w_], r=[K("mst")], w=[("mhg", gt)])

        for hh in range(2):
            P.op("pool", lambda e, hh=hh: e.collective_compute("AllGather", ALU.bypass, replica_groups=RG,
                                                               ins=[mhg_in[hh]], outs=[mhg_out[hh]]),
                 r=[("mhg", gt) for gt in range(NT)], w=[("mhg_out", hh)], kind="cc")

        P.barrier()
        if stop == "p2hg":
            return finish()
        NG = S // 512
        for hp in range(2):
            for G in range(NG):
                t0 = G * 512
                streams = [0, 1]
                for kb in range(4 * G + 3, -1, -1):
                    di = kb - 4 * G
                    c0 = max(0, di) * 128
                    def ctx(h2):
                        st = h2
                        return st, h2 * 64, hp * 2 + h2, ps[st * 3], ps[st * 3 + 1], ps[st * 3 + 2]
                    kq = [("QK", kb // 4), ("QK", G)]
                    vk = [("V", kb // 4)]
                    for h2 in streams:
                        st, r0, head, zp, pp, op_ = ctx(h2)
                        mm(zp[:, c0:512], QK[r0:r0 + 64, 2 + hp, kb * 128:(kb + 1) * 128], QK[r0:r0 + 64, hp, t0 + c0:t0 + 512],
                           True, True, r=kq, w=[("zp", st)])
                    for h2 in streams:
                        st, r0, head, zp, pp, op_ = ctx(h2)
                        act(E_[st][:, c0:512], zp[:, c0:512], AF.Exp, r=[("zp", st)], w=[("E", st)], scale=0.125)
                        if di >= 0:
                            vop("pool", "tensor_tensor", E_[st][:, c0:c0 + 128], E_[st][:, c0:c0 + 128], sbmask, ALU.mult,
                                r=[("E", st), "cb"], w=[("E", st)])
                    for h2 in streams:
                        st, r0, head, zp, pp, op_ = ctx(h2)
                        act(L_[st][:, c0:512], E_[st][:, c0:512], AF.Ln, r=[("E", st)], w=[("L", st)], bias=1.0)
                        mm(pp[:, c0:512], tri, L_[st][:, c0:512], kb == 4 * G + 3, True, r=["cb", ("L", st)], w=[("pp", st)])
                    for h2 in streams:
                        st, r0, head, zp, pp, op_ = ctx(h2)
                        act(X_[st][:, c0:512], pp[:, c0:512], AF.Exp, r=[("pp", st)], w=[("X", st)], scale=-1.0)
                        mm(pp[:, c0:512], omt, L_[st][:, c0:512], False, True, r=["cb", ("L", st), ("X", st)], w=[("pp", st)])
                        vop("dve", "tensor_tensor", Wt[st][:, c0:512], E_[st][:, c0:512], X_[st][:, c0:512], ALU.mult,
                            r=[("E", st), ("X", st)], w=[("W", st)])
                    for h2 in streams:
                        st, r0, head, zp, pp, op_ = ctx(h2)
                        vv = V[:, kb, head * 64:(head + 1) * 64]
                        mm(op_[0:64, c0:512], vv, Wt[st][:, c0:512], kb == 4 * G + 3, True, r=vk + [("W", st)], w=[("op", st)])
                if stop == "sbD":
                    return finish()
                for h2 in streams:
                    st = h2; head = hp * 2 + h2
                    op_ = ps[st * 3 + 2]; pn = ps[6 + st]
                    act(sq_[st], op_[0:64, :], AF.Square, r=[("op", st)], w=[("sq", st)])
                    mm(pn[0:64, :], ones_bf[0:64, 0:64], sq_[st], True, True, r=["ones", ("sq", st)], w=[("pn", st)])
                    act(rl_[st], pn[0:64, :], AF.Ln, r=[("pn", st)], w=[("rl", st)], scale=1.0 / 64, bias=EPS)
                    act(rs_[st], rl_[st], AF.Exp, r=[("rl", st)], w=[("rs", st)], scale=-0.5)
                    vop("dve", "scalar_tensor_tensor", mo_[st], op_[0:64, :], sbg[:, head:head + 1], rs_[st], ALU.mult, ALU.mult,
                        r=[("op", st), "sbg", ("rs", st)], w=[("mo", st)])
                    dma("sp", msb_in[t0 // T, head * 64:(head + 1) * 64, t0 % T:t0 % T + 512], mo_[st], r=[("mo", st)], w=[("msb", head, G)])

        if stop == "sbE":
            return finish()
        for hh in range(2):
            P.op("pool", lambda e, hh=hh: e.collective_compute("AllGather", ALU.bypass, replica_groups=RG,
                                                               ins=[msb_in[hh]], outs=[msb_out[hh]]),
                 r=[("msb", h, G) for h in range(4) for G in range(NG)], w=[("msb_out", hh)], kind="cc")

        P.barrier()
        if stop == "p2":
            return finish()

        A.reset()
        t = alloc_common()
        mixT = A.alloc([128, 8, CH], BF16)
        hgm = A.alloc([128, 4, 2, 256], BF16)
        memx = t["xres"]
        memT = A.alloc([128, 8, NMEM], BF16)
        kmT = A.alloc([128, 8, NMEM], BF16)
        vm = A.alloc([128, 2, D], BF16)
        qT = A.alloc([128, 8, CH], BF16)
        pb = A.alloc([128, 4, NMEM], BF16)
        pT = A.alloc([128, 8, 128], BF16)
        ob = A.alloc([128, D], BF16)
        oT = t["xnT"]
        mx = A.alloc([128, 4], F32); nb = A.alloc([128, 4], F32); rsum = A.alloc([128, 4], F32); rinv = A.alloc([128, 4], F32)
        yout = A.alloc([128, D], F32)
        vop("pool", "memset", t["mhalf"], -0.5, w=["mhalf"])

        load_gain(t, 0, 3)
        for tt in range(2):
            dma("sp", memx[:, tt, :], mem_d[tt * 128:(tt + 1) * 128, :], w=[("xres", tt)])
        t["memT"] = memT
        rms_T(t, 0, src="xres", ntt=2, dstT="memT")
        mT = [("memT", tt) for tt in range(2)]
        for g in range(2):
            wi = load_wg(t, wkv_s, g, wkeys("wkv"))
            for i4 in range(4):
                pp = ps[4 + i4 % 2]
                for kc in range(8):
                    mm(pp[:, 0:NMEM], t["wg"][wi][:, kc, i4 * 128:(i4 + 1) * 128], memT[:, kc, :], kc == 0, kc == 7,
                       r=[("wgt", wi)] + mT, w=[("ps", 4 + i4 % 2)])
                act(kmT[:, g * 4 + i4, :], pp[:, 0:NMEM], AF.Copy, r=[("ps", 4 + i4 % 2)], w=["kmT"])
        for g in range(2):
            wi = load_wg(t, wkv_s, 2 + g, wkeys("wkv"))
            for tt in range(2):
                pp = ps[4 + tt % 2]
                for kc in range(8):
                    mm(pp[:], memT[:, kc, tt * 128:(tt + 1) * 128], t["wg"][wi][:, kc, :], kc == 0, kc == 7,
                       r=[("wgt", wi)] + mT, w=[("ps", 4 + tt % 2)])
                act(vm[:, tt, g * 512:(g + 1) * 512], pp[:], AF.Copy, r=[("ps", 4 + tt % 2)], w=["vm"])
        load_gain(t, 0, 2)
        load_gain(t, 1, 4)
        load_gain(t, 2, 5)

        def fn_msb(e):
            return e.dma_start(out=msb_my.rearrange("(o r) t -> o r t", o=1), in_=msb_out[bass.ds(sp_par(e), 1), :, :])
        P.op("sp", fn_msb, w=["msb_my"], kind="dma")

        def fn_mhg(e):
            return e.dma_start(out=mhg_my.rearrange("(o m) t c -> o (m t) c", o=1), in_=mhg_out[bass.ds(sp_par(e), 1), :, :])
        P.op("sp", fn_mhg, w=["mhg_my"], kind="dma")

        for ci in range(NCH):
            for tt in range(4):
                dma("sp", t["xres"][:, tt, :], x1_s[ci * CH + tt * 128: ci * CH + (tt + 1) * 128, :],
                    r=[("x1s", ci, tt)], w=[("xres", tt)])
            for m in range(2):
                for j in range(2):
                    dma("sp", mixT[:, 4 + 2 * m + j, :], msb_my[m * 256 + j * 128: m * 256 + (j + 1) * 128, ci * CH:(ci + 1) * CH],
                        r=["msb_my"], w=[("mixT_sb", m, j)])
                dma("sp", hgm[:, :, m, :], mhg_my[m, ci * CH:(ci + 1) * CH, :].rearrange("(a p) c -> p a c", p=128),
                    r=["mhg_my"], w=[("hgm", m)])
            for tt in range(4):
                pst = ps[6 + tt % 2][:, 0:256].bitcast(BF16).rearrange("p (a b) -> p a b", a=4)
                for m in range(2):
                    for h in range(2):
                        tp(pst[:, 2 * m + h, :], hgm[:, tt, m, h * 128:(h + 1) * 128], r=[("hgm", m)], w=[("ps", 6 + tt % 2)])
                if tt % 2 == 0:
                    act(mixT[:, 0:4, tt * 128:(tt + 1) * 128], pst, AF.Copy, r=[("ps", 6 + tt % 2)], w=[("mixT_hg", tt)])
                else:
                    vop("dve", "tensor_copy", mixT[:, 0:4, tt * 128:(tt + 1) * 128], pst, r=[("ps", 6 + tt % 2)], w=[("mixT_hg", tt)])
            mixk = [("mixT_sb", m, j) for m in range(2) for j in range(2)]

            def proj_resid(srcT, srckeys, wsrc, wname):
                for half in range(2):
                    wi = load_wg(t, wsrc, half, wkeys(wname))
                    for tt in range(4):
                        pp = ps[4 + tt % 2]
                        for kc in range(8):
                            mm(pp[:], srcT[:, kc, tt * 128:(tt + 1) * 128], t["wg"][wi][:, kc, :], kc == 0, kc == 7,
                               r=[("wgt", wi)] + srckeys(tt), w=[("ps", 4 + tt % 2)])
                        vop("dve", "tensor_tensor", t["xres"][:, tt, half * 512:(half + 1) * 512], pp[:],
                            t["xres"][:, tt, half * 512:(half + 1) * 512], ALU.add, r=[("ps", 4 + tt % 2), ("xres", tt)], w=[("xres", tt)])

            proj_resid(mixT, lambda tt: mixk + [("mixT_hg", tt)], wout_s, "wout")
            rms_T(t, 0)
            xT = [("xnT", tt) for tt in range(4)]
            for g in range(2):
                wi = load_wg(t, wq_s, g, wkeys("wq"))
                for i4 in range(4):
                    pp = ps[4 + i4 % 2]
                    for kc in range(8):
                        mm(pp[:], t["wg"][wi][:, kc, i4 * 128:(i4 + 1) * 128], t["xnT"][:, kc, :], kc == 0, kc == 7,
                           r=[("wgt", wi)] + xT, w=[("ps", 4 + i4 % 2)])
                    if i4 % 2 == 0:
                        act(qT[:, g * 4 + i4, :], pp[:], AF.Copy, r=[("ps", 4 + i4 % 2)], w=[("qT", g * 4 + i4)])
                    else:
                        vop("dve", "tensor_copy", qT[:, g * 4 + i4, :], pp[:], r=[("ps", 4 + i4 % 2)], w=[("qT", g * 4 + i4)])
            for tt in range(4):
                psc = [ps[0], ps[1]]
                for h in range(4):
                    dst = psc[h // 2][:, (h % 2) * 256:(h % 2 + 1) * 256]
                    for j in range(2):
                        mm(dst, qT[:, 2 * h + j, tt * 128:(tt + 1) * 128], kmT[:, 2 * h + j, :], j == 0, j == 1,
                           r=[("qT", 2 * h + j), "kmT"], w=[("ps", h // 2)])
                for b2 in range(2):
                    vop("dve", "tensor_reduce", mx[:, 2 * b2:2 * b2 + 2], psc[b2][:].rearrange("p (h n) -> p h n", h=2),
                        AX.X, ALU.max, r=[("ps", b2)], w=[("mx", b2)])
                vop("dve", "tensor_scalar", nb, mx, -1.0 / 16, None, ALU.mult, r=[("mx", 0), ("mx", 1)], w=["nb"])
                for h in range(4):
                    src = psc[h // 2][:, (h % 2) * 256:(h % 2 + 1) * 256]
                    act(pb[:, h, :], src, AF.Exp, r=[("ps", h // 2), "nb"], w=[("pb", h)], scale=1.0 / 16,
                        bias=nb[:, h:h + 1], accum_out=rsum[:, h:h + 1])
                vop("dve", "reciprocal", rinv, rsum, r=[("pb", h) for h in range(4)], w=["rinv"])
                for half in range(2):
                    pst = ps[6 + half][:, 0:256].bitcast(BF16).rearrange("p (a b) -> p a b", a=4)
                    for j in range(4):
                        idx = half * 4 + j
                        tp(pst[:, j, :], pb[:, idx // 2, (idx % 2) * 128:(idx % 2 + 1) * 128], r=[("pb", idx // 2)], w=[("ps", 6 + half)])
                    if half == 0:
                        act(pT[:, 0:4, :], pst, AF.Copy, r=[("ps", 6 + half)], w=[("pT", half)])
                    else:
                        vop("dve", "tensor_copy", pT[:, 4:8, :], pst, r=[("ps", 6 + half)], w=[("pT", half)])
                pov = [ps[2], ps[3]]
                for h in range(4):
                    dst = pov[h // 2][:, (h % 2) * 256:(h % 2 + 1) * 256]
                    for j in range(2):
                        mm(dst, pT[:, 2 * h + j, :], vm[:, j, h * 256:(h + 1) * 256], j == 0, j == 1,
                           r=[("pT", h // 2), "vm"], w=[("ps", 2 + h // 2)])
                for h in range(4):
                    src = pov[h // 2][:, (h % 2) * 256:(h % 2 + 1) * 256]
                    if h // 2 == 0:
                        act(ob[:, h * 256:(h + 1) * 256], src, AF.Copy, r=[("ps", 2 + h // 2), "rinv"], w=[("ob", h)], scale=rinv[:, h:h + 1])
                    else:
                        vop("dve", "tensor_scalar", ob[:, h * 256:(h + 1) * 256], src, rinv[:, h:h + 1], None, ALU.mult,
                            r=[("ps", 2 + h // 2), "rinv"], w=[("ob", h)])
                for half in range(2):
                    pst = ps[6 + half][:, 0:256].bitcast(BF16).rearrange("p (a b) -> p a b", a=4)
                    for j in range(4):
                        kc = half * 4 + j
                        tp(pst[:, j, :], ob[:, kc * 128:(kc + 1) * 128], r=[("ob", kc // 2)], w=[("ps", 6 + half)])
                    if half == 0:
                        act(oT[:, 0:4, tt * 128:(tt + 1) * 128], pst, AF.Copy, r=[("ps", 6 + half)], w=[("xnT", tt)])
                    else:
                        vop("dve", "tensor_copy", oT[:, 4:8, tt * 128:(tt + 1) * 128], pst, r=[("ps", 6 + half)], w=[("xnT", tt)])
            proj_resid(oT, lambda tt: [("xnT", tt)], wo_s, "wo")
            rms_T(t, 1)
            ffn(t, 1, ci)
            for tt in range(4):
                act(t["junk"], t["xres"][:, tt, :], AF.Square, r=[("xres", tt)], w=["junk", ("ss", tt)], accum_out=t["ss"][:, tt:tt + 1])
            vop("pool", "tensor_scalar", t["rstd"][:, 0:4], t["ss"][:, 0:4], 1.0 / D, EPS, ALU.mult, ALU.add,
                r=[("ss", tt) for tt in range(4)], w=["rstd_a"])
            vop("pool", "tensor_tensor", t["rstd"][:, 0:4], t["rstd"][:, 0:4], t["mhalf"][:, 0:4], ALU.pow,
                r=["rstd_a", "mhalf"], w=["rstd"])
            for tt in range(4):
                vop("dve", "scalar_tensor_tensor", yout, t["xres"][:, tt, :], t["rstd"][:, tt:tt + 1], t["gain"][2],
                    ALU.mult, ALU.mult, r=[("xres", tt), "rstd", ("gain", 2)], w=["yout"])
                dma("sp", out_d[ci * CH + tt * 128: ci * CH + (tt + 1) * 128, :], yout, r=["yout"], w=[("out", ci, tt)])

        return finish()


_CACHE = {}


def _consts():
    i = np.arange(128)
    ident = np.eye(128, dtype=np.float32)
    tri = (i[:, None] >= i[None, :]).astype(np.float32)
    omt = 1.0 - tri
    sbmask = (i[:, None] < i[None, :]).astype(np.float32)
    causal = (i[:, None] <= i[None, :]).astype(np.float32)
    Mm = (i[:, None] <= i[None, :]).astype(np.float32) - (i[:, None] <= 63).astype(np.float32)
    ind = np.stack([(i <= 63), np.ones(128, bool), (i >= 64)], axis=1).astype(np.float32)
    cb = np.concatenate([ident, tri, omt, sbmask, causal, Mm, ind, np.zeros((128, 5), np.float32)], axis=1).astype(ml_dtypes.bfloat16)
    cf = np.concatenate([Mm, ind, np.zeros((128, 1), np.float32)], axis=1).astype(np.float32)
    return cb, cf


def _win_perm(j):
    HGW = 512
    def hgcols(part, m):
        return list(range(part * HGW + m * 256, part * HGW + (m + 1) * 256))
    def sbcols(part, heads):
        base = 4 * HGW + part * 512
        out = []
        for h in heads:
            out += list(range(base + h * 64, base + (h + 1) * 64))
        return out
    def qk(m):
        return (sbcols(0, [4 * m, 4 * m + 1]) + sbcols(0, [4 * m + 2, 4 * m + 3]) +
                sbcols(1, [4 * m, 4 * m + 1]) + sbcols(1, [4 * m + 2, 4 * m + 3]))
    def v(m):
        return sbcols(2, [4 * m, 4 * m + 1, 4 * m + 2, 4 * m + 3])
    me, pa = j, 1 - j
    cols = (qk(me) + qk(pa) + v(me) + hgcols(0, me) + hgcols(2, me) + hgcols(3, me) +
            hgcols(1, me) + hgcols(1, pa) + v(pa) + hgcols(0, pa) + hgcols(2, pa) + hgcols(3, pa))
    assert len(cols) == 3584
    return np.array(cols)


def _mix_perm():
    rows = []
    for m in range(2):
        rows += list(range(m * 256, (m + 1) * 256))
    for m in range(2):
        rows += list(range(512 + m * 256, 512 + (m + 1) * 256))
    return np.array(rows)


def make_in_maps(inputs, S):
    T = S // 2
    f = lambda a: np.ascontiguousarray(np.asarray(a, dtype=np.float32))
    x = f(inputs["x"]); mem = f(inputs["mem"])
    B = x.shape[0]
    cb, cf = _consts()
    gains = np.stack([f(inputs["ffn1_norm"])[0], f(inputs["mix_norm"])[0], f(inputs["mem_q_norm"])[0],
                      f(inputs["mem_kv_norm"])[0], f(inputs["ffn2_norm"])[0], f(inputs["final_norm"])], axis=0)
    w_in = f(inputs["w_in"])[0]
    w_out = f(inputs["w_out"])[0][_mix_perm(), :]
    lbr = f(inputs["hg_lb_raw"]); hgn = f(inputs["hg_gnorm"])[0]; sbn = f(inputs["sb_gnorm"])[0]
    maps = []
    for c in range(2 * B):
        b, j = c // 2, c % 2
        hgp = np.stack([lbr[0, j * 256:(j + 1) * 256], lbr[1, j * 256:(j + 1) * 256], hgn[j * 256:(j + 1) * 256]], axis=0)
        sbg = np.ascontiguousarray(sbn[j * 256:(j + 1) * 256].reshape(4, 64).T)
        maps.append({
            "x": np.ascontiguousarray(x[b, j * T:(j + 1) * T]), "mem": np.ascontiguousarray(mem[b]),
            "w_gu1": f(inputs["ffn1_w_gu"])[0], "w_gu2": f(inputs["ffn2_w_gu"])[0],
            "w_d1": f(inputs["ffn1_w_down"])[0], "w_d2": f(inputs["ffn2_w_down"])[0],
            "w_in": np.ascontiguousarray(w_in[:, _win_perm(j)]), "w_out": np.ascontiguousarray(w_out),
            "w_q": f(inputs["mem_w_q"])[0], "w_kv": f(inputs["mem_w_kv"])[0], "w_o": f(inputs["mem_w_o"])[0],
            "gains": np.ascontiguousarray(gains), "hgp": np.ascontiguousarray(hgp), "sbg": sbg, "cb": cb, "cf": cf,
        })
    return maps


def kernel(**inputs):
    x = np.asarray(inputs["x"])
    B, S, _ = x.shape
    T = S // 2
    import os
    if S not in _CACHE:
        _CACHE[S] = build(S, stop=os.environ.get("K_STOP"))
    nc = _CACHE[S]
    maps = make_in_maps(inputs, S)
    res = run_bass_kernel_spmd(nc, maps, core_ids=list(range(2 * B)))
    out = np.empty((B, S, D), np.float32)
    for c in range(2 * B):
        out[c // 2, (c % 2) * T:(c % 2 + 1) * T] = res.results[c]["out"]
    return out
```

```python
import contextlib
import numpy as np
import ml_dtypes
import concourse.bass as bass
import concourse.mybir as mybir
from concourse.bass_utils import run_bass_kernel_spmd

F32 = mybir.dt.float32
BF16 = mybir.dt.bfloat16
AF = mybir.ActivationFunctionType
ALU = mybir.AluOpType
AX = mybir.AxisListType

D = 1024
DFF = 2816
NHC = 22
NMEM = 256
EPS = 1e-6
CH = 512
PKW = 8192

ENGS = ["pe", "act", "dve", "pool", "sp"]
SAME_ENGINE_SYNC = True


class Op:
    __slots__ = ("eng", "fn", "kind", "deps", "signal", "sig_n", "dma_i", "cc_i", "idx")


class Prog:
    NS = 8
    EP = 8000

    def __init__(self, nc):
        self.nc = nc
        self.q = {e: [] for e in ENGS}
        self.lastw = {}
        self.rd = {}
        self.ndma = {"sp": 0, "pool": 0, "act": 0}
        self.ncc = 0
        self.all_dma = {"sp": [], "pool": [], "act": []}
        self.all_cc = []

    def op(self, eng, fn, r=(), w=(), kind="c"):
        o = Op()
        o.eng = eng; o.fn = fn; o.kind = kind; o.deps = set(); o.signal = False
        o.sig_n = 0; o.dma_i = -1; o.cc_i = -1
        for k in r:
            d = self.lastw.get(k)
            if d is not None:
                o.deps.add(d)
        for k in w:
            d = self.lastw.get(k)
            if d is not None:
                o.deps.add(d)
            for x in self.rd.get(k, {}).values():
                o.deps.add(x)
        o.deps.discard(o)
        for k in w:
            self.lastw[k] = o
            self.rd[k] = {}
        wset = set(w)
        for k in r:
            if k not in wset:
                self.rd.setdefault(k, {})[eng] = o
        if kind == "dma":
            o.dma_i = self.ndma[eng]; self.ndma[eng] += 1
            self.all_dma[eng].append(o)
        elif kind == "cc":
            o.cc_i = self.ncc; self.ncc += 1
            self.all_cc.append(o)
        o.idx = len(self.q[eng])
        self.q[eng].append(o)
        return o

    def barrier(self):
        last = []
        for e in ENGS:
            for o in reversed(self.q[e]):
                if o.kind == "c" and o.fn is not None:
                    last.append(o); break
        for e in self.all_dma:
            last.extend(self.all_dma[e][-self.NS:])
        last.extend(self.all_cc)
        for e in ENGS:
            o = Op()
            o.eng = e; o.fn = None; o.kind = "c"; o.deps = set(last); o.signal = False
            o.sig_n = 0; o.dma_i = -1; o.cc_i = -1; o.idx = len(self.q[e])
            self.q[e].append(o)
        self.lastw = {}
        self.rd = {}

    def emit(self):
        nc = self.nc
        for e in ENGS:
            for o in self.q[e]:
                for d in o.deps:
                    if d.kind == "c":
                        if d.eng == e and (e == "pe" or not SAME_ENGINE_SYNC):
                            continue
                        d.signal = True
        nsig = {}
        for e in ENGS:
            n = 0
            for o in self.q[e]:
                if o.kind == "c" and o.signal:
                    n += 1; o.sig_n = n
            nsig[e] = n
        csem = {e: [nc.alloc_semaphore(f"c_{e}_{k}") for k in range(nsig[e] // self.EP + 1)] for e in ENGS}
        dsem = {e: [nc.alloc_semaphore(f"d_{e}_{k}") for k in range(self.NS)] for e in self.all_dma}
        ccsem = [nc.alloc_semaphore(f"cc_{k}") for k in range(self.ncc)]
        endsem = nc.alloc_semaphore("endsem")
        allsems = [x for e in ENGS for x in csem[e]] + [x for e in dsem for x in dsem[e]] + ccsem
        NS, EP = self.NS, self.EP

        def target(d):
            if d.kind == "c":
                n = d.sig_n - 1
                return csem[d.eng][n // EP], n % EP + 1
            if d.kind == "dma":
                return dsem[d.eng][d.dma_i % NS], 16 * (d.dma_i // NS + 1)
            return ccsem[d.cc_i], 1

        def run(e, engine):
            waited = {}

            def wait(sem, val):
                key = id(sem)
                if waited.get(key, 0) >= val:
                    return
                waited[key] = val
                engine.wait_ge(sem, val)

            for o in self.q[e]:
                need = {}
                for d in o.deps:
                    if d.kind == "c" and d.eng == e and (e == "pe" or not SAME_ENGINE_SYNC):
                        continue
                    s, v = target(d)
                    k = id(s)
                    if k not in need or need[k][1] < v:
                        need[k] = (s, v)
                if o.kind == "dma" and o.dma_i >= NS:
                    s = dsem[e][o.dma_i % NS]; v = 16 * (o.dma_i // NS)
                    k = id(s)
                    if k not in need or need[k][1] < v:
                        need[k] = (s, v)
                for s, v in need.values():
                    wait(s, v)
                if o.fn is None:
                    continue
                ins = o.fn(engine)
                if o.kind == "c":
                    if o.signal:
                        n = o.sig_n - 1
                        ins.then_inc(csem[e][n // EP], 1)
                elif o.kind == "dma":
                    ins.then_inc(dsem[e][o.dma_i % NS], 16)
                else:
                    ins.then_inc(ccsem[o.cc_i], 1)

        with nc.Block() as block:
            @block.tensor
            def _(t):
                run("pe", t)

            @block.scalar
            def _(s):
                run("act", s)

            @block.vector
            def _(v):
                run("dve", v)

            @block.gpsimd
            def _(g):
                run("pool", g)

            @block.sync
            def _(s):
                run("sp", s)


class Arena:
    def __init__(self, ap, size):
        self.ap = ap; self.size = size; self.off = 0

    def reset(self):
        self.off = 0

    def alloc(self, shape, dt):
        n = int(np.prod(shape[1:]))
        nb = n * (4 if dt == F32 else 2)
        nb = (nb + 63) // 64 * 64
        ne = nb // 2
        assert self.off + ne <= self.size, ("arena overflow", self.off, ne, self.size)
        v = self.ap[0:shape[0], self.off:self.off + ne]
        self.off += ne
        if dt == F32:
            v = v.bitcast(F32)[:, 0:n]
        else:
            v = v[:, 0:n]
        if len(shape) == 3:
            v = v.rearrange("p (a b) -> p a b", a=shape[1])
        elif len(shape) == 4:
            v = v.rearrange("p (a b c) -> p a b c", a=shape[1], b=shape[2])
        return v


def build(S, debug=False, stop=None):
    T = S // 2
    NCH = T // CH
    NT = S // 128
    nc = bass.Bass("TRN2", target_bir_lowering=False)
    P = Prog(nc)

    def din(name, shape, dt=F32):
        return nc.dram_tensor(name, shape, dt, kind="ExternalInput").ap()

    x_d = din("x", [T, D]); mem_d = din("mem", [NMEM, D])
    wgu_d = [din("w_gu1", [D, 2 * DFF]), din("w_gu2", [D, 2 * DFF])]
    wd_d = [din("w_d1", [DFF, D]), din("w_d2", [DFF, D])]
    win_d = din("w_in", [D, 3584]); wout_d = din("w_out", [D, D])
    wq_d = din("w_q", [D, D]); wkv_d = din("w_kv", [D, 2 * D]); wo_d = din("w_o", [D, D])
    gains_d = din("gains", [6, D])
    hgp_d = din("hgp", [3, 256])
    sbg_d = din("sbg", [64, 4])
    cb_d = din("cb", [128, 776], BF16)
    cf_d = din("cf", [128, 128 + 4])
    out_d = nc.dram_tensor("out", [T, D], F32, kind="ExternalOutput").ap()

    dbg_list = []

    def dscr(name, shape, dt=BF16):
        a = nc.dram_tensor(name, shape, dt, kind="Internal").ap()
        if debug and name in debug:
            dbg_list.append((a, nc.dram_tensor("dbg_" + name, shape, dt, kind="ExternalOutput").ap()))
        return a

    def finish():
        P.barrier()
        for a, b in dbg_list:
            P.op("sp", lambda e, a=a, b=b: e.dma_start(out=b, in_=a), kind="dma")
        P.barrier()
        P.emit()
        return nc

    wgu_s = [dscr(f"wgu_s{i}", [NHC, 128, 8, 2, 128]) for i in range(2)]
    wd_s = [dscr(f"wd_s{i}", [NHC, 128, D]) for i in range(2)]
    win_s = dscr("win_s", [7, 128, 8, 512])
    wout_s = dscr("wout_s", [2, 128, 8, 512])
    wq_s = dscr("wq_s", [2, 128, 8, 512])
    wkv_s = dscr("wkv_s", [4, 128, 8, 512])
    wo_s = dscr("wo_s", [2, 128, 8, 512])
    x1_s = dscr("x1_s", [T, D], F32)
    big = dscr("big", [NCH, 3 * 128, PKW])
    ain = dscr("ain", [NCH, 128, PKW])
    seq = dscr("seq", [2, NCH, 128, PKW])
    msb_my = dscr("msb_my", [512, T]); mhg_my = dscr("mhg_my", [2, T, 256])
    mhg_in = dscr("mhg_in", [2, T, 256]); mhg_out = dscr("mhg_out", [2, 2 * T, 256])
    msb_in = dscr("msb_in", [2, 256, T]); msb_out = dscr("msb_out", [2, 512, T])
    RG = [[0, 1], [2, 3], [4, 5], [6, 7]]

    es = contextlib.ExitStack()
    with es:
        ARENA_E = 94 * 1024
        arena_t = es.enter_context(nc.sbuf_tensor("arena", [128, ARENA_E], BF16))
        A = Arena(arena_t, ARENA_E)
        cb = es.enter_context(nc.sbuf_tensor("cb_sb", [128, 776], BF16))
        cf = es.enter_context(nc.sbuf_tensor("cf_sb", [128, 132], F32))
        ident = cb[:, 0:128]; tri = cb[:, 128:256]; omt = cb[:, 256:384]
        sbmask = cb[:, 384:512]; causal = cb[:, 512:640]
        Mm = cb[:, 640:768]; ind3 = cb[:, 768:772]
        ones_bf = es.enter_context(nc.sbuf_tensor("ones_bf", [128, 128], BF16))
        ps = [es.enter_context(nc.psum_tensor(f"ps{i}", [128, 512], F32)) for i in range(8)]

        def dma(eng, out, in_, r=(), w=(), **kw):
            return P.op(eng, lambda e, out=out, in_=in_, kw=kw: e.dma_start(out=out, in_=in_, **kw), r=r, w=w, kind="dma")

        def mm(out, lhsT, rhs, start, stop, r=(), w=()):
            return P.op("pe", lambda e: e.matmul(out, lhsT, rhs, start=start, stop=stop), r=r, w=w)

        def tp(out, in_, r=(), w=()):
            return P.op("pe", lambda e: e.transpose(out, in_, ident), r=list(r) + ["cb"], w=w)

        def act(out, in_, func, r=(), w=(), **kw):
            return P.op("act", lambda e: e.activation(out, in_, func, **kw), r=r, w=w)

        def vop(eng, name, *args, r=(), w=(), **kw):
            return P.op(eng, lambda e: getattr(e, name)(*args, **kw), r=r, w=w)

        dma("sp", cb[:], cb_d, w=["cb"])
        dma("sp", cf[:], cf_d, w=["cf"])
        vop("pool", "memset", ones_bf[:], 1.0, w=["ones"])

        def prep_w(src, dst, ngrp, gw, name):
            for kc in range(8):
                dma("pool", dst[:, :, kc, :], src[kc * 128:(kc + 1) * 128, :].rearrange("p (g c) -> g p c", c=gw),
                    w=[(name, kc)])

        def prep_ffn(i):
            for kc in range(8):
                for gu in range(2):
                    dma("pool", wgu_s[i][:, :, kc, gu, :],
                        wgu_d[i][kc * 128:(kc + 1) * 128, gu * DFF:(gu + 1) * DFF].rearrange("p (h c) -> h p c", c=128),
                        w=[(f"wgu{i}", kc, gu)])
            for q4 in range(2):
                dma("pool", wd_s[i][q4 * 11:(q4 + 1) * 11],
                    wd_d[i][q4 * 11 * 128:(q4 + 1) * 11 * 128, :].rearrange("(h p) c -> h p c", p=128), w=[(f"wd{i}", q4)])

        def wgu_keys(i):
            return [(f"wgu{i}", kc, gu) for kc in range(8) for gu in range(2)]

        def wd_keys(i):
            return [(f"wd{i}", q4) for q4 in range(2)]

        def wkeys(name):
            return [(name, kc) for kc in range(8)]

        prep_ffn(0)
        prep_w(win_d, win_s, 7, 512, "win")
        prep_w(wout_d, wout_s, 2, 512, "wout")
        prep_w(wq_d, wq_s, 2, 512, "wq")
        prep_w(wkv_d, wkv_s, 4, 512, "wkv")
        prep_w(wo_d, wo_s, 2, 512, "wo")
        prep_ffn(1)

        if stop == "p0":
            return finish()
        def alloc_common():
            t = {}
            t["xres"] = A.alloc([128, 4, D], F32)
            t["xn"] = A.alloc([128, 4, D], BF16)
            t["junk"] = A.alloc([128, D], BF16)
            t["xnT"] = A.alloc([128, 8, CH], BF16)
            t["hT"] = A.alloc([128, NHC, CH], BF16)
            t["wgu"] = [A.alloc([128, 8, 2, 128], BF16) for _ in range(3)]
            t["wd"] = A.alloc([128, NHC, D], BF16)
            t["wg"] = [A.alloc([128, 8, 512], BF16) for _ in range(2)]
            t["gain"] = [A.alloc([128, D], F32) for _ in range(3)]
            t["ss"] = A.alloc([128, 8], F32)
            t["rstd"] = A.alloc([128, 8], F32)
            t["mhalf"] = A.alloc([128, 8], F32)
            t["sg"] = [A.alloc([128, CH], BF16) for _ in range(2)]
            return t

        state = {"wgu_i": 0, "wg_i": 0}

        def load_gain(t, slot, row):
            dma("sp", t["gain"][slot], gains_d[row:row + 1, :].to_broadcast([128, D]), w=[("gain", slot)])

        def rms_T(t, gslot, src="xres", ntt=4, dstT="xnT"):
            xres = t[src]
            for tt in range(ntt):
                act(t["junk"], xres[:, tt, :], AF.Square, r=[(src, tt)], w=["junk", ("ss", tt)], accum_out=t["ss"][:, tt:tt + 1])
            vop("pool", "tensor_scalar", t["rstd"][:, 0:ntt], t["ss"][:, 0:ntt], 1.0 / D, EPS, ALU.mult, ALU.add,
                r=[("ss", tt) for tt in range(ntt)], w=["rstd_a"])
            vop("pool", "tensor_tensor", t["rstd"][:, 0:ntt], t["rstd"][:, 0:ntt], t["mhalf"][:, 0:ntt], ALU.pow,
                r=["rstd_a", "mhalf"], w=["rstd"])
            for tt in range(ntt):
                vop("dve", "scalar_tensor_tensor", t["xn"][:, tt, :], xres[:, tt, :], t["rstd"][:, tt:tt + 1],
                    t["gain"][gslot], ALU.mult, ALU.mult, r=[(src, tt), "rstd", ("gain", gslot)], w=[("xn", tt)])
            for tt in range(ntt):
                for half in range(2):
                    pst = ps[6 + half][:, 0:256].bitcast(BF16).rearrange("p (a b) -> p a b", a=4)
                    for j in range(4):
                        kc = half * 4 + j
                        tp(pst[:, j, :], t["xn"][:, tt, kc * 128:(kc + 1) * 128], r=[("xn", tt)], w=[("ps", 6 + half)])
                    eng = "act" if half == 0 else "dve"
                    if eng == "act":
                        act(t[dstT][:, half * 4:half * 4 + 4, tt * 128:(tt + 1) * 128], pst, AF.Copy,
                            r=[("ps", 6 + half)], w=[(dstT, tt)])
                    else:
                        vop("dve", "tensor_copy", t[dstT][:, half * 4:half * 4 + 4, tt * 128:(tt + 1) * 128], pst,
                            r=[("ps", 6 + half)], w=[(dstT, tt)])

        def ffn(t, i, ci):
            for q4 in range(2):
                dma("sp", t["wd"][:, q4 * 11:(q4 + 1) * 11, :], wd_s[i][q4 * 11:(q4 + 1) * 11].rearrange("h p c -> p h c"),
                    r=wd_keys(i), w=[("wdt", q4)])
            xT = [("xnT", tt) for tt in range(4)]
            for hc in range(NHC):
                wi = state["wgu_i"] % 3; state["wgu_i"] += 1
                wt = t["wgu"][wi]
                dma("sp", wt, wgu_s[i][hc], r=wgu_keys(i), w=[("wgut", wi)])
                pg = ps[(hc % 2) * 2]; pu = ps[(hc % 2) * 2 + 1]
                for gu, pp in ((0, pg), (1, pu)):
                    for kc in range(8):
                        mm(pp[:], wt[:, kc, gu, :], t["xnT"][:, kc, :], kc == 0, kc == 7,
                           r=[("wgut", wi)] + xT, w=[("ps", (hc % 2) * 2 + gu)])
                sg = t["sg"][hc % 2]
                act(sg, pg[:], AF.Silu, r=[("ps", (hc % 2) * 2)], w=[("sg", hc % 2)])
                vop("dve", "tensor_tensor", t["hT"][:, hc, :], sg, pu[:], ALU.mult,
                    r=[("sg", hc % 2), ("ps", (hc % 2) * 2 + 1)], w=[("hT", hc)])
            hk = [("hT", hc) for hc in range(NHC)]
            n = 0
            for tt in range(4):
                for half in range(2):
                    bk = 4 + n % 2; pp = ps[bk]; n += 1
                    for hc in range(NHC):
                        mm(pp[:], t["hT"][:, hc, tt * 128:(tt + 1) * 128], t["wd"][:, hc, half * 512:(half + 1) * 512],
                           hc == 0, hc == NHC - 1, r=[("hT", hc), ("wdt", hc // 11)], w=[("ps", bk)])
                    vop("dve", "scalar_tensor_tensor", t["xres"][:, tt, half * 512:(half + 1) * 512], pp[:], 0.5,
                        t["xres"][:, tt, half * 512:(half + 1) * 512], ALU.mult, ALU.add,
                        r=[("ps", bk), ("xres", tt)], w=[("xres", tt)])

        def load_wg(t, src, g, keys):
            wi = state["wg_i"] % 2; state["wg_i"] += 1
            dma("sp", t["wg"][wi], src[g], r=keys, w=[("wgt", wi)])
            return wi

        A.reset()
        t = alloc_common()
        pk = [A.alloc([128, PKW], BF16) for _ in range(2)]
        vop("pool", "memset", t["mhalf"], -0.5, w=["mhalf"])
        load_gain(t, 0, 0); load_gain(t, 1, 1)
        par_holder = {}

        def sp_par(e):
            if "v" not in par_holder:
                par_holder["v"] = e.partition_id() % 2
            return par_holder["v"]

        for ci in range(NCH):
            for tt in range(4):
                dma("sp", t["xres"][:, tt, :], x_d[ci * CH + tt * 128: ci * CH + (tt + 1) * 128, :], w=[("xres", tt)])
            if stop == "p1x":
                return finish()
            rms_T(t, 0)
            if stop == "p1a":
                return finish()
            ffn(t, 0, ci)
            if stop == "p1b":
                return finish()
            for tt in range(4):
                dma("sp", x1_s[ci * CH + tt * 128: ci * CH + (tt + 1) * 128, :], t["xres"][:, tt, :], r=[("xres", tt)],
                    w=[("x1s", ci, tt)])
            if stop == "p1b1":
                return finish()
            rms_T(t, 1)
            if stop == "p1b2":
                return finish()
            xT = [("xnT", tt) for tt in range(4)]
            for g in range(2):
                wi = load_wg(t, win_s, g, wkeys("win"))
                for i4 in range(4):
                    pp = ps[4 + i4 % 2]
                    for kc in range(8):
                        mm(pp[:], t["wg"][wi][:, kc, i4 * 128:(i4 + 1) * 128], t["xnT"][:, kc, :], kc == 0, kc == 7,
                           r=[("wgt", wi)] + xT, w=[("ps", 4 + i4 % 2)])
                    if i4 % 2 == 0:
                        act(pk[g][:, i4 * 512:(i4 + 1) * 512], pp[:], AF.Copy, r=[("ps", 4 + i4 % 2)], w=[("pk", g)])
                    else:
                        vop("dve", "tensor_copy", pk[g][:, i4 * 512:(i4 + 1) * 512], pp[:], r=[("ps", 4 + i4 % 2)], w=[("pk", g)])
            if stop == "p1b3":
                return finish()
            for g in range(2, 7):
                if stop == f"p1g{g}":
                    return finish()
                wi = load_wg(t, win_s, g, wkeys("win"))
                for tt in range(4):
                    pp = ps[4 + tt % 2]
                    for kc in range(8):
                        mm(pp[:], t["xnT"][:, kc, tt * 128:(tt + 1) * 128], t["wg"][wi][:, kc, :], kc == 0, kc == 7,
                           r=[("wgt", wi), ("xnT", tt)], w=[("ps", 4 + tt % 2)])
                    if g == 4:
                        for m in range(2):
                            dst = pk[m][:, 6144 + tt * 512: 6144 + (tt + 1) * 512].bitcast(F32)
                            if tt % 2 == 0:
                                act(dst, pp[:, m * 256:(m + 1) * 256], AF.Copy, r=[("ps", 4 + tt % 2)], w=[("pk", m)])
                            else:
                                vop("dve", "tensor_copy", dst, pp[:, m * 256:(m + 1) * 256], r=[("ps", 4 + tt % 2)], w=[("pk", m)])
                    else:
                        m = 0 if g < 4 else 1
                        base = 2048 if g in (2, 5) else 4096
                        dstv = pk[m][:, base:base + 2048].rearrange("p (s a c) -> p s a c", s=2, a=4)[:, :, tt, :]
                        src = pp[:].rearrange("p (s c) -> p s c", s=2)
                        if tt % 2 == 0:
                            act(dstv, src, AF.Copy, r=[("ps", 4 + tt % 2)], w=[("pk", m)])
                        else:
                            vop("dve", "tensor_copy", dstv, src, r=[("ps", 4 + tt % 2)], w=[("pk", m)])
            if stop == "p1c":
                return finish()
            dma("sp", big[ci, 256:384, :], pk[0], r=[("pk", 0)], w=[("big", ci, 2)])
            dma("sp", ain[ci], pk[1], r=[("pk", 1)], w=[("ain", ci)])
            if stop == "p1d":
                return finish()
            P.op("pool", lambda e, ci=ci: e.collective_compute("AllGather", ALU.bypass, replica_groups=RG,
                                                               ins=[ain[ci]], outs=[big[ci, 0:256, :]]),
                 r=[("ain", ci)], w=[("big", ci, 0), ("big", ci, 1)], kind="cc")

        P.barrier()
        if stop == "p1":
            return finish()
        if debug:
            early = nc.dram_tensor("dbg_x1_early", [T, D], F32, kind="ExternalOutput").ap()
            P.op("sp", lambda e: e.dma_start(out=early, in_=x1_s), kind="dma")
            P.barrier()

        A.reset()
        QK = A.alloc([128, 4, S], BF16)
        V = A.alloc([128, NT, 256], BF16)
        hq = [A.alloc([128, 4, 256], BF16) for _ in range(2)]
        hi = [A.alloc([128, 4, 256], BF16) for _ in range(2)]
        hg = [A.alloc([128, 4, 256], BF16) for _ in range(2)]
        hfr = [A.alloc([128, 2048], BF16) for _ in range(2)]
        hf = [x.bitcast(F32).rearrange("p (a c) -> p a c", a=4) for x in hfr]
        lbr = A.alloc([128, 3, 256], F32)
        lb = A.alloc([128, 256], F32); oml = A.alloc([128, 256], F32); gnb = A.alloc([128, 256], F32)
        sbg = A.alloc([64, 4], F32)
        Sst = [A.alloc([128, 128], F32) for _ in range(2)]
        Stmp = [A.alloc([128, 128], F32) for _ in range(2)]
        Sbf = [A.alloc([128, 128], BF16) for _ in range(2)]

        def W2(n, shape=(128, 256), dt=F32):
            return [A.alloc(list(shape), dt) for _ in range(n)]
        hg_mark = A.off
        lhi = W2(2, dt=BF16); llo = W2(2, dt=BF16); e1 = W2(2); f_ = W2(2); logf = W2(2); kk = W2(2); eq = W2(2); ek = W2(2); e2 = W2(2); sgg = W2(2)
        qh = W2(2, dt=BF16); kh = W2(2, dt=BF16)
        qhT = W2(2, (128, 2, 128), BF16); khT0 = W2(2, (128, 2, 128), BF16); khT1 = W2(2, (128, 2, 128), BF16)
        am = W2(2, (128, 2, 128), BF16)
        esc = W2(2, (128, 2, 3), F32)
        hss = W2(2, (128, 2), F32); hln = W2(2, (128, 2), F32); hrs = W2(2, (128, 2), F32)
        mst = W2(2, (128, 256), BF16)
        junk2 = A.alloc([128, 128], BF16)

        if stop == "p2a0":
            return finish()
        if stop == "p2a":
            return finish()

        big4 = big.rearrange("n (s p) c -> n s p c", s=3)
        for hfi in range(2):
            def fnq(e, hfi=hfi):
                par = sp_par(e)
                idx = ((1 - par) * 2) if hfi == 0 else (par + 1)
                return e.dma_start(out=seq[hfi], in_=big4[:, bass.ds(idx, 1), :, :].rearrange("n o p c -> (n o) p c"))
            P.op("sp", fnq, w=[("seq", hfi)], kind="dma")

        if stop == "p2b":
            return finish()

        def slot_ap(ci, hfi, c0, c1):
            return seq[hfi, ci][:, c0:c1]

        def dyn_dma(out, src, r=(), w=(), rearr=None, hfi=0):
            if rearr is not None:
                src = rearr(src)
            return dma("sp", out, src, r=list(r) + [("seq", hfi)], w=w)

        dma("sp", lbr.rearrange("p a c -> p (a c)"), hgp_d.rearrange("a c -> (a c)").rearrange("(o n) -> o n", o=1).to_broadcast([128, 768]), w=["lbr"])
        dma("sp", sbg, sbg_d, w=["sbg"])
        vop("dve", "tensor_tensor", lb, lbr[:, 1, :], lbr[:, 0, :], ALU.subtract, r=["lbr"], w=["lb0"])
        act(lb, lb, AF.Exp, r=["lb0"], w=["lb1"])
        vop("dve", "tensor_scalar", lb, lb, 1.0, None, ALU.add, r=["lb1"], w=["lb2"])
        vop("dve", "reciprocal", lb, lb, r=["lb2"], w=["lb"])
        vop("dve", "tensor_scalar", oml, lb, -1.0, 1.0, ALU.mult, ALU.add, r=["lb"], w=["oml"])
        vop("dve", "tensor_copy", gnb, lbr[:, 2, :], r=["lbr"], w=["gnb"])
        for h in range(2):
            vop("dve", "memset", Sst[h], 0.0, w=[("S", h)])
            for i in range(2):
                vop("pool", "memset", khT0[i][:, h, :], 0.0, w=[("khT0", i, h)])
                vop("pool", "memset", khT1[i][:, h, :], 0.0, w=[("khT1", i, h)])

        if stop == "p2c":
            return finish()
        for hfi in range(2):
            for ci in range(NCH):
                n0 = hfi * T + ci * CH
                deps = [("big", ci, 0), ("big", ci, 1), ("big", ci, 2)]
                dyn_dma(QK[:, :, n0:n0 + CH], slot_ap(ci, hfi, 0, 2048), w=[("QK", n0 // CH)],
                        rearr=lambda s: s.rearrange("p (i t) -> p i t", i=4), hfi=hfi)
                dyn_dma(V[:, n0 // 128:n0 // 128 + 4, :], slot_ap(ci, hfi, 2048, 3072), w=[("V", n0 // CH)],
                        rearr=lambda s: s.rearrange("p (a c) -> p a c", a=4), hfi=hfi)

        if stop == "p2l":
            return finish()
        mhg_v = mhg_in.rearrange("h (n p) c -> (h n) p c", p=128)
        for hfi in range(2):
            for ci in range(NCH):
                bi = (hfi * NCH + ci) % 2
                dyn_dma(hq[bi], slot_ap(ci, hfi, 3072, 4096), w=[("hq", bi)], rearr=lambda s: s.rearrange("p (a c) -> p a c", a=4), hfi=hfi)
                dyn_dma(hi[bi], slot_ap(ci, hfi, 4096, 5120), w=[("hi", bi)], rearr=lambda s: s.rearrange("p (a c) -> p a c", a=4), hfi=hfi)
                dyn_dma(hg[bi], slot_ap(ci, hfi, 5120, 6144), w=[("hg", bi)], rearr=lambda s: s.rearrange("p (a c) -> p a c", a=4), hfi=hfi)
                dyn_dma(hfr[bi], slot_ap(ci, hfi, 6144, 8192), w=[("hf", bi)], hfi=hfi)
                if stop == "hgA0":
                    return finish()
                for tt in range(4):
                    gt = (hfi * T + ci * CH) // 128 + tt
                    w_ = gt % 2
                    K = lambda name: (name, w_)
                    act(e1[w_], hf[bi][:, tt, :], AF.Exp, r=[("hf", bi)], w=[K("e1")], scale=-1.0)
                    if stop == "hgA1":
                        return finish()
                    vop("dve", "tensor_scalar", e1[w_], e1[w_], 1.0, None, ALU.add, r=[K("e1")], w=[K("e1")])
                    if stop == "hgA2":
                        return finish()
                    vop("dve", "reciprocal", e1[w_], e1[w_], r=[K("e1")], w=[K("e1")])
                    if stop == "hgA3":
                        return finish()
                    vop("dve", "tensor_tensor", f_[w_], e1[w_], oml, ALU.mult, r=[K("e1"), "oml"], w=[K("f")])
                    vop("dve", "tensor_tensor", f_[w_], f_[w_], lb, ALU.add, r=[K("f"), "lb"], w=[K("f")])
                    act(logf[w_], f_[w_], AF.Ln, r=[K("f")], w=[K("logf")])
                    vop("dve", "tensor_scalar", kk[w_], f_[w_], -1.0, 1.0, ALU.mult, ALU.add, r=[K("f")], w=[K("kk")])
                    if stop == "hgA":
                        return finish()
                    vop("pool", "tensor_copy", lhi[w_], logf[w_], r=[K("logf")], w=[K("lhi")])
                    vop("pool", "tensor_tensor", llo[w_], logf[w_], lhi[w_], ALU.subtract, r=[K("logf"), K("lhi")], w=[K("llo")])
                    mm(ps[0][:, 0:256], Mm, lhi[w_], True, False, r=["cb", K("lhi")], w=["p_c1"])
                    mm(ps[0][:, 0:256], Mm, llo[w_], False, True, r=["cb", K("llo")], w=["p_c1"])
                    for h in range(2):
                        mm(ps[1][:, h * 4:h * 4 + 4], lhi[w_][:, h * 128:(h + 1) * 128], ind3, True, False,
                           r=["cb", K("lhi")], w=["p_sc"])
                        mm(ps[1][:, h * 4:h * 4 + 4], llo[w_][:, h * 128:(h + 1) * 128], ind3, False, True,
                           r=["cb", K("llo")], w=["p_sc"])
                    act(eq[w_], ps[0][:, 0:256], AF.Exp, r=["p_c1"], w=[K("eq")])
                    act(ek[w_], ps[0][:, 0:256], AF.Exp, r=["p_c1"], w=[K("ek")], scale=-1.0)
                    act(esc[w_], ps[1][:, 0:8].rearrange("p (h c) -> p h c", h=2)[:, :, 0:3], AF.Exp, r=["p_sc"], w=[K("esc")])
                    vop("dve", "tensor_tensor", qh[w_], hq[bi][:, tt, :], eq[w_], ALU.mult, r=[("hq", bi), K("eq")], w=[K("qh")])
                    vop("dve", "tensor_tensor", kh[w_], kk[w_], ek[w_], ALU.mult, r=[K("kk"), K("ek")], w=[K("kh")])
                    if stop == "hgB":
                        return finish()
                    act(e2[w_], hg[bi][:, tt, :], AF.Exp, r=[("hg", bi)], w=[K("e2")], scale=-1.0)
                    vop("dve", "tensor_scalar", e2[w_], e2[w_], 1.0, None, ALU.add, r=[K("e2")], w=[K("e2")])
                    vop("dve", "reciprocal", e2[w_], e2[w_], r=[K("e2")], w=[K("e2")])
                    vop("dve", "tensor_tensor", sgg[w_], e2[w_], hg[bi][:, tt, :], ALU.mult, r=[K("e2"), ("hg", bi)], w=[K("sgg")])
                    vop("dve", "tensor_tensor", sgg[w_], sgg[w_], gnb, ALU.mult, r=[K("sgg"), "gnb"], w=[K("sgg")])
                    ptq = ps[2][:, 0:128].bitcast(BF16).rearrange("p (h c) -> p h c", h=2)
                    ptk = ps[3][:, 0:128].bitcast(BF16).rearrange("p (h c) -> p h c", h=2)
                    for h in range(2):
                        tp(ptq[:, h, :], qh[w_][:, h * 128:(h + 1) * 128], r=[K("qh")], w=["ptq"])
                        tp(ptk[:, h, :], kh[w_][:, h * 128:(h + 1) * 128], r=[K("kh")], w=["ptk"])
                    act(qhT[w_], ptq, AF.Copy, r=["ptq"], w=[K("qhT")])
                    vop("dve", "tensor_copy", khT0[w_][:, :, 0:64], ptk[:, :, 0:64], r=["ptk"], w=[K("khT0")])
                    vop("dve", "tensor_copy", khT1[w_][:, :, 64:128], ptk[:, :, 64:128], r=["ptk"], w=[K("khT1")])
                    if stop == "hgC":
                        return finish()
                    for h in range(2):
                        pa = ps[4][:, h * 128:(h + 1) * 128]
                        mm(pa, khT0[w_][:, h, :], qhT[w_][:, h, :], True, False, r=[K("khT0"), K("qhT")], w=[("p_a", h)])
                        mm(pa[:, 64:128], khT1[w_][:, h, :], qhT[w_][:, h, 64:128], False, True, r=[K("khT1"), K("qhT")], w=[("p_a", h)])
                        vop("dve", "tensor_tensor", am[w_][:, h, :], pa, causal, ALU.mult, r=[("p_a", h), "cb"], w=[K("am") + (h,)])
                        act(Sbf[h], Sst[h], AF.Copy, r=[("S", h), K("esc")], w=[("Sbf", h)], scale=esc[w_][:, h, 0:1])
                        vop("dve", "tensor_scalar", Stmp[h], Sst[h], esc[w_][:, h, 1:2], None, ALU.mult,
                            r=[("S", h), K("esc")], w=[("Stmp", h)])
                        po = ps[5][:, h * 128:(h + 1) * 128]
                        ih = hi[bi][:, tt, h * 128:(h + 1) * 128]
                        mm(po, am[w_][:, h, :], ih, True, False, r=[K("am") + (h,), ("hi", bi)], w=[("p_o", h)])
                        mm(po, qhT[w_][:, h, :], Sbf[h], False, True, r=[K("qhT"), ("Sbf", h)], w=[("p_o", h)])
                        pu = ps[6][:, h * 128:(h + 1) * 128]
                        mm(pu, kh[w_][:, h * 128:(h + 1) * 128], ih, True, True, r=[K("kh"), ("hi", bi)], w=[("p_u", h)])
                        vop("dve", "scalar_tensor_tensor", Sst[h], pu, esc[w_][:, h, 2:3], Stmp[h], ALU.mult, ALU.add,
                            r=[("p_u", h), K("esc"), ("Stmp", h)], w=[("S", h)])
                        act(junk2, po, AF.Square, r=[("p_o", h)], w=["junk2", K("hss") + (h,)], accum_out=hss[w_][:, h:h + 1])
                    if stop == "hgD":
                        return finish()
                    act(hln[w_], hss[w_], AF.Ln, r=[K("hss") + (0,), K("hss") + (1,)], w=[K("hln")], scale=1.0 / 128, bias=EPS)
                    act(hrs[w_], hln[w_], AF.Exp, r=[K("hln")], w=[K("hrs")], scale=-0.5)
                    for h in range(2):
                        po = ps[5][:, h * 128:(h + 1) * 128]
                        vop("dve", "scalar_tensor_tensor", mst[w_][:, h * 128:(h + 1) * 128], po, hrs[w_][:, h:h + 1],
                            sgg[w_][:, h * 128:(h + 1) * 128], ALU.mult, ALU.mult,
                            r=[("p_o", h), K("hrs"), K("sgg")], w=[K("mst")])
                    dma("sp", mhg_v[gt], mst[w_], r=[K("mst")], w=[("mhg", gt)])

        for hh in range(2):
            P.op("pool", lambda e, hh=hh: e.collective_compute("AllGather", ALU.bypass, replica_groups=RG,
                                                               ins=[mhg_in[hh]], outs=[mhg_out[hh]]),
                 r=[("mhg", gt) for gt in range(NT)], w=[("mhg_out", hh)], kind="cc")

        P.barrier()
        if stop == "p2hg":
            return finish()
        A.off = hg_mark
        KBB = 4
        E_ = [W2(KBB, (128, 512), F32) for _ in range(2)]; L_ = [W2(KBB, (128, 512), BF16) for _ in range(2)]; X_ = W2(2, (128, 512), F32); Wt = W2(2, (128, 512), BF16)
        sq_ = W2(2, (64, 512), BF16); rl_ = W2(2, (64, 512), F32); rs_ = W2(2, (64, 512), F32); mo_ = W2(2, (64, 512), BF16)
        NG = S // 512
        for hp in range(2):
            for G in range(NG):
                t0 = G * 512
                streams = [0, 1]
                kbs = list(range(4 * G + 3, -1, -1))
                for b0 in range(0, len(kbs), KBB):
                    batch = kbs[b0:b0 + KBB]
                    for sl, kb in enumerate(batch):
                        di = kb - 4 * G
                        c0 = max(0, di) * 128
                        kq = [("QK", kb // 4), ("QK", G)]
                        for st in streams:
                            r0 = st * 64
                            zb = (0 if st == 0 else 3) if sl % 2 == 0 else 6 + st
                            zp = ps[zb]
                            mm(zp[:, c0:512], QK[r0:r0 + 64, 2 + hp, kb * 128:(kb + 1) * 128], QK[r0:r0 + 64, hp, t0 + c0:t0 + 512],
                               True, True, r=kq, w=[("psb", zb)])
                            act(E_[st][sl][:, c0:512], zp[:, c0:512], AF.Exp, r=[("psb", zb)], w=[("E", st, sl)], scale=0.125)
                            if di >= 0:
                                vop("pool", "tensor_tensor", E_[st][sl][:, c0:c0 + 128], E_[st][sl][:, c0:c0 + 128], sbmask, ALU.mult,
                                    r=[("E", st, sl), "cb"], w=[("E", st, sl)])
                    for sl, kb in enumerate(batch):
                        c0 = max(0, kb - 4 * G) * 128
                        for st in streams:
                            act(L_[st][sl][:, c0:512], E_[st][sl][:, c0:512], AF.Ln, r=[("E", st, sl)], w=[("L", st, sl)], bias=1.0)
                    for sl, kb in enumerate(batch):
                        c0 = max(0, kb - 4 * G) * 128
                        first = kb == 4 * G + 3
                        vk = [("V", kb // 4)]
                        for st in streams:
                            pp = ps[st * 3 + 1]
                            mm(pp[:, c0:512], tri, L_[st][sl][:, c0:512], first, True, r=["cb", ("L", st, sl)], w=[("pp", st)])
                        for st in streams:
                            pp = ps[st * 3 + 1]
                            act(X_[st][:, c0:512], pp[:, c0:512], AF.Exp, r=[("pp", st)], w=[("X", st)], scale=-1.0)
                            mm(pp[:, c0:512], omt, L_[st][sl][:, c0:512], False, True, r=["cb", ("L", st, sl), ("X", st)], w=[("pp", st)])
                            vop("dve", "tensor_tensor", Wt[st][:, c0:512], E_[st][sl][:, c0:512], X_[st][:, c0:512], ALU.mult,
                                r=[("E", st, sl), ("X", st)], w=[("W", st)])
                        for st in streams:
                            head = hp * 2 + st
                            op_ = ps[st * 3 + 2]
                            vv = V[:, kb, head * 64:(head + 1) * 64]
                            mm(op_[0:64, c0:512], vv, Wt[st][:, c0:512], first, True, r=vk + [("W", st)], w=[("op", st)])
                if stop == "sbD":
                    return finish()
                for h2 in streams:
                    st = h2; head = hp * 2 + h2
                    op_ = ps[st * 3 + 2]; pn = ps[6 + st]
                    act(sq_[st], op_[0:64, :], AF.Square, r=[("op", st)], w=[("sq", st)])
                    mm(pn[0:64, :], ones_bf[0:64, 0:64], sq_[st], True, True, r=["ones", ("sq", st)], w=[("psb", 6 + st)])
                    act(rl_[st], pn[0:64, :], AF.Ln, r=[("psb", 6 + st)], w=[("rl", st)], scale=1.0 / 64, bias=EPS)
                    act(rs_[st], rl_[st], AF.Exp, r=[("rl", st)], w=[("rs", st)], scale=-0.5)
                    vop("dve", "scalar_tensor_tensor", mo_[st], op_[0:64, :], sbg[:, head:head + 1], rs_[st], ALU.mult, ALU.mult,
                        r=[("op", st), "sbg", ("rs", st)], w=[("mo", st)])
                    dma("sp", msb_in[t0 // T, head * 64:(head + 1) * 64, t0 % T:t0 % T + 512], mo_[st], r=[("mo", st)], w=[("msb", head, G)])

        if stop == "sbE":
            return finish()
        for hh in range(2):
            P.op("pool", lambda e, hh=hh: e.collective_compute("AllGather", ALU.bypass, replica_groups=RG,
                                                               ins=[msb_in[hh]], outs=[msb_out[hh]]),
                 r=[("msb", h, G) for h in range(4) for G in range(NG)], w=[("msb_out", hh)], kind="cc")

        P.barrier()
        if stop == "p2":
            return finish()

        A.reset()
        t = alloc_common()
        mixT = A.alloc([128, 8, CH], BF16)
        hgm = A.alloc([128, 4, 2, 256], BF16)
        memx = t["xres"]
        memT = A.alloc([128, 8, NMEM], BF16)
        kmT = A.alloc([128, 8, NMEM], BF16)
        vm = A.alloc([128, 2, D], BF16)
        qT = A.alloc([128, 8, CH], BF16)
        pb = A.alloc([128, 4, NMEM], BF16)
        pT = A.alloc([128, 8, 128], BF16)
        ob = A.alloc([128, D], BF16)
        oT = t["xnT"]
        mx = A.alloc([128, 4], F32); nb = A.alloc([128, 4], F32); rsum = A.alloc([128, 4], F32); rinv = A.alloc([128, 4], F32)
        yout = A.alloc([128, D], F32)
        vop("pool", "memset", t["mhalf"], -0.5, w=["mhalf"])

        load_gain(t, 0, 3)
        for tt in range(2):
            dma("sp", memx[:, tt, :], mem_d[tt * 128:(tt + 1) * 128, :], w=[("xres", tt)])
        t["memT"] = memT
        rms_T(t, 0, src="xres", ntt=2, dstT="memT")
        mT = [("memT", tt) for tt in range(2)]
        for g in range(2):
            wi = load_wg(t, wkv_s, g, wkeys("wkv"))
            for i4 in range(4):
                pp = ps[4 + i4 % 2]
                for kc in range(8):
                    mm(pp[:, 0:NMEM], t["wg"][wi][:, kc, i4 * 128:(i4 + 1) * 128], memT[:, kc, :], kc == 0, kc == 7,
                       r=[("wgt", wi)] + mT, w=[("ps", 4 + i4 % 2)])
                act(kmT[:, g * 4 + i4, :], pp[:, 0:NMEM], AF.Copy, r=[("ps", 4 + i4 % 2)], w=["kmT"])
        for g in range(2):
            wi = load_wg(t, wkv_s, 2 + g, wkeys("wkv"))
            for tt in range(2):
                pp = ps[4 + tt % 2]
                for kc in range(8):
                    mm(pp[:], memT[:, kc, tt * 128:(tt + 1) * 128], t["wg"][wi][:, kc, :], kc == 0, kc == 7,
                       r=[("wgt", wi)] + mT, w=[("ps", 4 + tt % 2)])
                act(vm[:, tt, g * 512:(g + 1) * 512], pp[:], AF.Copy, r=[("ps", 4 + tt % 2)], w=["vm"])
        load_gain(t, 0, 2)
        load_gain(t, 1, 4)
        load_gain(t, 2, 5)

        def fn_msb(e):
            return e.dma_start(out=msb_my.rearrange("(o r) t -> o r t", o=1), in_=msb_out[bass.ds(sp_par(e), 1), :, :])
        P.op("sp", fn_msb, w=["msb_my"], kind="dma")

        def fn_mhg(e):
            return e.dma_start(out=mhg_my.rearrange("(o m) t c -> o (m t) c", o=1), in_=mhg_out[bass.ds(sp_par(e), 1), :, :])
        P.op("sp", fn_mhg, w=["mhg_my"], kind="dma")

        for ci in range(NCH):
            for tt in range(4):
                dma("sp", t["xres"][:, tt, :], x1_s[ci * CH + tt * 128: ci * CH + (tt + 1) * 128, :],
                    r=[("x1s", ci, tt)], w=[("xres", tt)])
            for m in range(2):
                for j in range(2):
                    dma("sp", mixT[:, 4 + 2 * m + j, :], msb_my[m * 256 + j * 128: m * 256 + (j + 1) * 128, ci * CH:(ci + 1) * CH],
                        r=["msb_my"], w=[("mixT_sb", m, j)])
                dma("sp", hgm[:, :, m, :], mhg_my[m, ci * CH:(ci + 1) * CH, :].rearrange("(a p) c -> p a c", p=128),
                    r=["mhg_my"], w=[("hgm", m)])
            for tt in range(4):
                pst = ps[6 + tt % 2][:, 0:256].bitcast(BF16).rearrange("p (a b) -> p a b", a=4)
                for m in range(2):
                    for h in range(2):
                        tp(pst[:, 2 * m + h, :], hgm[:, tt, m, h * 128:(h + 1) * 128], r=[("hgm", m)], w=[("ps", 6 + tt % 2)])
                if tt % 2 == 0:
                    act(mixT[:, 0:4, tt * 128:(tt + 1) * 128], pst, AF.Copy, r=[("ps", 6 + tt % 2)], w=[("mixT_hg", tt)])
                else:
                    vop("dve", "tensor_copy", mixT[:, 0:4, tt * 128:(tt + 1) * 128], pst, r=[("ps", 6 + tt % 2)], w=[("mixT_hg", tt)])
            mixk = [("mixT_sb", m, j) for m in range(2) for j in range(2)]

            def proj_resid(srcT, srckeys, wsrc, wname):
                for half in range(2):
                    wi = load_wg(t, wsrc, half, wkeys(wname))
                    for tt in range(4):
                        pp = ps[4 + tt % 2]
                        for kc in range(8):
                            mm(pp[:], srcT[:, kc, tt * 128:(tt + 1) * 128], t["wg"][wi][:, kc, :], kc == 0, kc == 7,
                               r=[("wgt", wi)] + srckeys(tt), w=[("ps", 4 + tt % 2)])
                        vop("dve", "tensor_tensor", t["xres"][:, tt, half * 512:(half + 1) * 512], pp[:],
                            t["xres"][:, tt, half * 512:(half + 1) * 512], ALU.add, r=[("ps", 4 + tt % 2), ("xres", tt)], w=[("xres", tt)])

            proj_resid(mixT, lambda tt: mixk + [("mixT_hg", tt)], wout_s, "wout")
            rms_T(t, 0)
            xT = [("xnT", tt) for tt in range(4)]
            for g in range(2):
                wi = load_wg(t, wq_s, g, wkeys("wq"))
                for i4 in range(4):
                    pp = ps[4 + i4 % 2]
                    for kc in range(8):
                        mm(pp[:], t["wg"][wi][:, kc, i4 * 128:(i4 + 1) * 128], t["xnT"][:, kc, :], kc == 0, kc == 7,
                           r=[("wgt", wi)] + xT, w=[("ps", 4 + i4 % 2)])
                    if i4 % 2 == 0:
                        act(qT[:, g * 4 + i4, :], pp[:], AF.Copy, r=[("ps", 4 + i4 % 2)], w=[("qT", g * 4 + i4)])
                    else:
                        vop("dve", "tensor_copy", qT[:, g * 4 + i4, :], pp[:], r=[("ps", 4 + i4 % 2)], w=[("qT", g * 4 + i4)])
            for tt in range(4):
                psc = [ps[0], ps[1]]
                for h in range(4):
                    dst = psc[h // 2][:, (h % 2) * 256:(h % 2 + 1) * 256]
                    for j in range(2):
                        mm(dst, qT[:, 2 * h + j, tt * 128:(tt + 1) * 128], kmT[:, 2 * h + j, :], j == 0, j == 1,
                           r=[("qT", 2 * h + j), "kmT"], w=[("ps", h // 2)])
                for b2 in range(2):
                    vop("dve", "tensor_reduce", mx[:, 2 * b2:2 * b2 + 2], psc[b2][:].rearrange("p (h n) -> p h n", h=2),
                        AX.X, ALU.max, r=[("ps", b2)], w=[("mx", b2)])
                vop("dve", "tensor_scalar", nb, mx, -1.0 / 16, None, ALU.mult, r=[("mx", 0), ("mx", 1)], w=["nb"])
                for h in range(4):
                    src = psc[h // 2][:, (h % 2) * 256:(h % 2 + 1) * 256]
                    act(pb[:, h, :], src, AF.Exp, r=[("ps", h // 2), "nb"], w=[("pb", h)], scale=1.0 / 16,
                        bias=nb[:, h:h + 1], accum_out=rsum[:, h:h + 1])
                vop("dve", "reciprocal", rinv, rsum, r=[("pb", h) for h in range(4)], w=["rinv"])
                for half in range(2):
                    pst = ps[6 + half][:, 0:256].bitcast(BF16).rearrange("p (a b) -> p a b", a=4)
                    for j in range(4):
                        idx = half * 4 + j
                        tp(pst[:, j, :], pb[:, idx // 2, (idx % 2) * 128:(idx % 2 + 1) * 128], r=[("pb", idx // 2)], w=[("ps", 6 + half)])
                    if half == 0:
                        act(pT[:, 0:4, :], pst, AF.Copy, r=[("ps", 6 + half)], w=[("pT", half)])
                    else:
                        vop("dve", "tensor_copy", pT[:, 4:8, :], pst, r=[("ps", 6 + half)], w=[("pT", half)])
                pov = [ps[2], ps[3]]
                for h in range(4):
                    dst = pov[h // 2][:, (h % 2) * 256:(h % 2 + 1) * 256]
                    for j in range(2):
                        mm(dst, pT[:, 2 * h + j, :], vm[:, j, h * 256:(h + 1) * 256], j == 0, j == 1,
                           r=[("pT", h // 2), "vm"], w=[("ps", 2 + h // 2)])
                for h in range(4):
                    src = pov[h // 2][:, (h % 2) * 256:(h % 2 + 1) * 256]
                    if h // 2 == 0:
                        act(ob[:, h * 256:(h + 1) * 256], src, AF.Copy, r=[("ps", 2 + h // 2), "rinv"], w=[("ob", h)], scale=rinv[:, h:h + 1])
                    else:
                        vop("dve", "tensor_scalar", ob[:, h * 256:(h + 1) * 256], src, rinv[:, h:h + 1], None, ALU.mult,
                            r=[("ps", 2 + h // 2), "rinv"], w=[("ob", h)])
                for half in range(2):
                    pst = ps[6 + half][:, 0:256].bitcast(BF16).rearrange("p (a b) -> p a b", a=4)
                    for j in range(4):
                        kc = half * 4 + j
                        tp(pst[:, j, :], ob[:, kc * 128:(kc + 1) * 128], r=[("ob", kc // 2)], w=[("ps", 6 + half)])
                    if half == 0:
                        act(oT[:, 0:4, tt * 128:(tt + 1) * 128], pst, AF.Copy, r=[("ps", 6 + half)], w=[("xnT", tt)])
                    else:
                        vop("dve", "tensor_copy", oT[:, 4:8, tt * 128:(tt + 1) * 128], pst, r=[("ps", 6 + half)], w=[("xnT", tt)])
            proj_resid(oT, lambda tt: [("xnT", tt)], wo_s, "wo")
            rms_T(t, 1)
            ffn(t, 1, ci)
            for tt in range(4):
                act(t["junk"], t["xres"][:, tt, :], AF.Square, r=[("xres", tt)], w=["junk", ("ss", tt)], accum_out=t["ss"][:, tt:tt + 1])
            vop("pool", "tensor_scalar", t["rstd"][:, 0:4], t["ss"][:, 0:4], 1.0 / D, EPS, ALU.mult, ALU.add,
                r=[("ss", tt) for tt in range(4)], w=["rstd_a"])
            vop("pool", "tensor_tensor", t["rstd"][:, 0:4], t["rstd"][:, 0:4], t["mhalf"][:, 0:4], ALU.pow,
                r=["rstd_a", "mhalf"], w=["rstd"])
            for tt in range(4):
                vop("dve", "scalar_tensor_tensor", yout, t["xres"][:, tt, :], t["rstd"][:, tt:tt + 1], t["gain"][2],
                    ALU.mult, ALU.mult, r=[("xres", tt), "rstd", ("gain", 2)], w=["yout"])
                dma("sp", out_d[ci * CH + tt * 128: ci * CH + (tt + 1) * 128, :], yout, r=["yout"], w=[("out", ci, tt)])

        return finish()


_CACHE = {}


def _consts():
    i = np.arange(128)
    ident = np.eye(128, dtype=np.float32)
    tri = (i[:, None] >= i[None, :]).astype(np.float32)
    omt = 1.0 - tri
    sbmask = (i[:, None] < i[None, :]).astype(np.float32)
    causal = (i[:, None] <= i[None, :]).astype(np.float32)
    Mm = (i[:, None] <= i[None, :]).astype(np.float32) - (i[:, None] <= 63).astype(np.float32)
    ind = np.stack([(i <= 63), np.ones(128, bool), (i >= 64)], axis=1).astype(np.float32)
    cb = np.concatenate([ident, tri, omt, sbmask, causal, Mm, ind, np.zeros((128, 5), np.float32)], axis=1).astype(ml_dtypes.bfloat16)
    cf = np.concatenate([Mm, ind, np.zeros((128, 1), np.float32)], axis=1).astype(np.float32)
    return cb, cf


def _win_perm(j):
    HGW = 512
    def hgcols(part, m):
        return list(range(part * HGW + m * 256, part * HGW + (m + 1) * 256))
    def sbcols(part, heads):
        base = 4 * HGW + part * 512
        out = []
        for h in heads:
            out += list(range(base + h * 64, base + (h + 1) * 64))
        return out
    def qk(m):
        return (sbcols(0, [4 * m, 4 * m + 1]) + sbcols(0, [4 * m + 2, 4 * m + 3]) +
                sbcols(1, [4 * m, 4 * m + 1]) + sbcols(1, [4 * m + 2, 4 * m + 3]))
    def v(m):
        return sbcols(2, [4 * m, 4 * m + 1, 4 * m + 2, 4 * m + 3])
    me, pa = j, 1 - j
    cols = (qk(me) + qk(pa) + v(me) + hgcols(0, me) + hgcols(2, me) + hgcols(3, me) +
            hgcols(1, me) + hgcols(1, pa) + v(pa) + hgcols(0, pa) + hgcols(2, pa) + hgcols(3, pa))
    assert len(cols) == 3584
    return np.array(cols)


def _mix_perm():
    rows = []
    for m in range(2):
        rows += list(range(m * 256, (m + 1) * 256))
    for m in range(2):
        rows += list(range(512 + m * 256, 512 + (m + 1) * 256))
    return np.array(rows)


def make_in_maps(inputs, S):
    T = S // 2
    f = lambda a: np.ascontiguousarray(np.asarray(a, dtype=np.float32))
    x = f(inputs["x"]); mem = f(inputs["mem"])
    B = x.shape[0]
    cb, cf = _consts()
    gains = np.stack([f(inputs["ffn1_norm"])[0], f(inputs["mix_norm"])[0], f(inputs["mem_q_norm"])[0],
                      f(inputs["mem_kv_norm"])[0], f(inputs["ffn2_norm"])[0], f(inputs["final_norm"])], axis=0)
    w_in = f(inputs["w_in"])[0]
    w_out = f(inputs["w_out"])[0][_mix_perm(), :]
    lbr = f(inputs["hg_lb_raw"]); hgn = f(inputs["hg_gnorm"])[0]; sbn = f(inputs["sb_gnorm"])[0]
    maps = []
    for c in range(2 * B):
        b, j = c // 2, c % 2
        hgp = np.stack([lbr[0, j * 256:(j + 1) * 256], lbr[1, j * 256:(j + 1) * 256], hgn[j * 256:(j + 1) * 256]], axis=0)
        sbg = np.ascontiguousarray(sbn[j * 256:(j + 1) * 256].reshape(4, 64).T)
        maps.append({
            "x": np.ascontiguousarray(x[b, j * T:(j + 1) * T]), "mem": np.ascontiguousarray(mem[b]),
            "w_gu1": f(inputs["ffn1_w_gu"])[0], "w_gu2": f(inputs["ffn2_w_gu"])[0],
            "w_d1": f(inputs["ffn1_w_down"])[0], "w_d2": f(inputs["ffn2_w_down"])[0],
            "w_in": np.ascontiguousarray(w_in[:, _win_perm(j)]), "w_out": np.ascontiguousarray(w_out),
            "w_q": f(inputs["mem_w_q"])[0], "w_kv": f(inputs["mem_w_kv"])[0], "w_o": f(inputs["mem_w_o"])[0],
            "gains": np.ascontiguousarray(gains), "hgp": np.ascontiguousarray(hgp), "sbg": sbg, "cb": cb, "cf": cf,
        })
    return maps


def kernel(**inputs):
    x = np.asarray(inputs["x"])
    B, S, _ = x.shape
    T = S // 2
    import os
    if S not in _CACHE:
        _CACHE[S] = build(S, stop=os.environ.get("K_STOP"))
    nc = _CACHE[S]
    maps = make_in_maps(inputs, S)
    res = run_bass_kernel_spmd(nc, maps, core_ids=list(range(2 * B)))
    out = np.empty((B, S, D), np.float32)
    for c in range(2 * B):
        out[c // 2, (c % 2) * T:(c % 2 + 1) * T] = res.results[c]["out"]
    return out
```

```python
import contextlib
import numpy as np
import ml_dtypes
import concourse.bass as bass
import concourse.mybir as mybir
from concourse.bass_utils import run_bass_kernel_spmd

F32 = mybir.dt.float32
BF16 = mybir.dt.bfloat16
AF = mybir.ActivationFunctionType
ALU = mybir.AluOpType
AX = mybir.AxisListType

D = 1024
DFF = 2816
NHC = 22
NMEM = 256
EPS = 1e-6
CH = 512
PKW = 8192

ENGS = ["pe", "act", "dve", "pool", "sp"]
SAME_ENGINE_SYNC = True


class Op:
    __slots__ = ("eng", "fn", "kind", "deps", "signal", "sig_n", "dma_i", "cc_i", "idx")


class Prog:
    NS = 8
    EP = 8000

    def __init__(self, nc):
        self.nc = nc
        self.q = {e: [] for e in ENGS}
        self.lastw = {}
        self.rd = {}
        self.ndma = {"sp": 0, "pool": 0, "act": 0}
        self.ncc = 0
        self.all_dma = {"sp": [], "pool": [], "act": []}
        self.all_cc = []

    def op(self, eng, fn, r=(), w=(), kind="c"):
        o = Op()
        o.eng = eng; o.fn = fn; o.kind = kind; o.deps = set(); o.signal = False
        o.sig_n = 0; o.dma_i = -1; o.cc_i = -1
        for k in r:
            d = self.lastw.get(k)
            if d is not None:
                o.deps.add(d)
        for k in w:
            d = self.lastw.get(k)
            if d is not None:
                o.deps.add(d)
            for x in self.rd.get(k, {}).values():
                o.deps.add(x)
        o.deps.discard(o)
        for k in w:
            self.lastw[k] = o
            self.rd[k] = {}
        wset = set(w)
        for k in r:
            if k not in wset:
                self.rd.setdefault(k, {})[eng] = o
        if kind == "dma":
            o.dma_i = self.ndma[eng]; self.ndma[eng] += 1
            self.all_dma[eng].append(o)
        elif kind == "cc":
            o.cc_i = self.ncc; self.ncc += 1
            self.all_cc.append(o)
        o.idx = len(self.q[eng])
        self.q[eng].append(o)
        return o

    def barrier(self):
        last = []
        for e in ENGS:
            for o in reversed(self.q[e]):
                if o.kind == "c" and o.fn is not None:
                    last.append(o); break
        for e in self.all_dma:
            last.extend(self.all_dma[e][-self.NS:])
        last.extend(self.all_cc)
        for e in ENGS:
            o = Op()
            o.eng = e; o.fn = None; o.kind = "c"; o.deps = set(last); o.signal = False
            o.sig_n = 0; o.dma_i = -1; o.cc_i = -1; o.idx = len(self.q[e])
            self.q[e].append(o)
        self.lastw = {}
        self.rd = {}

    def emit(self):
        nc = self.nc
        for e in ENGS:
            for o in self.q[e]:
                for d in o.deps:
                    if d.kind == "c":
                        if d.eng == e and (e == "pe" or not SAME_ENGINE_SYNC):
                            continue
                        d.signal = True
        nsig = {}
        for e in ENGS:
            n = 0
            for o in self.q[e]:
                if o.kind == "c" and o.signal:
                    n += 1; o.sig_n = n
            nsig[e] = n
        csem = {e: [nc.alloc_semaphore(f"c_{e}_{k}") for k in range(nsig[e] // self.EP + 1)] for e in ENGS}
        dsem = {e: [nc.alloc_semaphore(f"d_{e}_{k}") for k in range(self.NS)] for e in self.all_dma}
        ccsem = [nc.alloc_semaphore(f"cc_{k}") for k in range(self.ncc)]
        endsem = nc.alloc_semaphore("endsem")
        allsems = [x for e in ENGS for x in csem[e]] + [x for e in dsem for x in dsem[e]] + ccsem
        NS, EP = self.NS, self.EP

        def target(d):
            if d.kind == "c":
                n = d.sig_n - 1
                return csem[d.eng][n // EP], n % EP + 1
            if d.kind == "dma":
                return dsem[d.eng][d.dma_i % NS], 16 * (d.dma_i // NS + 1)
            return ccsem[d.cc_i], 1

        def run(e, engine):
            waited = {}

            def wait(sem, val):
                key = id(sem)
                if waited.get(key, 0) >= val:
                    return
                waited[key] = val
                engine.wait_ge(sem, val)

            for o in self.q[e]:
                need = {}
                for d in o.deps:
                    if d.kind == "c" and d.eng == e and (e == "pe" or not SAME_ENGINE_SYNC):
                        continue
                    s, v = target(d)
                    k = id(s)
                    if k not in need or need[k][1] < v:
                        need[k] = (s, v)
                if o.kind == "dma" and o.dma_i >= NS:
                    s = dsem[e][o.dma_i % NS]; v = 16 * (o.dma_i // NS)
                    k = id(s)
                    if k not in need or need[k][1] < v:
                        need[k] = (s, v)
                for s, v in need.values():
                    wait(s, v)
                if o.fn is None:
                    continue
                ins = o.fn(engine)
                if o.kind == "c":
                    if o.signal:
                        n = o.sig_n - 1
                        ins.then_inc(csem[e][n // EP], 1)
                elif o.kind == "dma":
                    ins.then_inc(dsem[e][o.dma_i % NS], 16)
                else:
                    ins.then_inc(ccsem[o.cc_i], 1)

        with nc.Block() as block:
            @block.tensor
            def _(t):
                run("pe", t)

            @block.scalar
            def _(s):
                run("act", s)

            @block.vector
            def _(v):
                run("dve", v)

            @block.gpsimd
            def _(g):
                run("pool", g)

            @block.sync
            def _(s):
                run("sp", s)


class Arena:
    def __init__(self, ap, size):
        self.ap = ap; self.size = size; self.off = 0

    def reset(self):
        self.off = 0

    def alloc(self, shape, dt):
        n = int(np.prod(shape[1:]))
        nb = n * (4 if dt == F32 else 2)
        nb = (nb + 63) // 64 * 64
        ne = nb // 2
        assert self.off + ne <= self.size, ("arena overflow", self.off, ne, self.size)
        v = self.ap[0:shape[0], self.off:self.off + ne]
        self.off += ne
        if dt == F32:
            v = v.bitcast(F32)[:, 0:n]
        else:
            v = v[:, 0:n]
        if len(shape) == 3:
            v = v.rearrange("p (a b) -> p a b", a=shape[1])
        elif len(shape) == 4:
            v = v.rearrange("p (a b c) -> p a b c", a=shape[1], b=shape[2])
        return v


def build(S, debug=False, stop=None):
    T = S // 2
    NCH = T // CH
    NT = S // 128
    nc = bass.Bass("TRN2", target_bir_lowering=False)
    P = Prog(nc)

    def din(name, shape, dt=F32):
        return nc.dram_tensor(name, shape, dt, kind="ExternalInput").ap()

    x_d = din("x", [T, D]); mem_d = din("mem", [NMEM, D])
    wgu_d = [din("w_gu1", [D, 2 * DFF]), din("w_gu2", [D, 2 * DFF])]
    wd_d = [din("w_d1", [DFF, D]), din("w_d2", [DFF, D])]
    win_d = din("w_in", [D, 3584]); wout_d = din("w_out", [D, D])
    wq_d = din("w_q", [D, D]); wkv_d = din("w_kv", [D, 2 * D]); wo_d = din("w_o", [D, D])
    gains_d = din("gains", [6, D])
    hgp_d = din("hgp", [3, 256])
    sbg_d = din("sbg", [64, 4])
    cb_d = din("cb", [128, 776], BF16)
    cf_d = din("cf", [128, 128 + 4])
    out_d = nc.dram_tensor("out", [T, D], F32, kind="ExternalOutput").ap()

    dbg_list = []

    def dscr(name, shape, dt=BF16):
        a = nc.dram_tensor(name, shape, dt, kind="Internal").ap()
        if debug and name in debug:
            dbg_list.append((a, nc.dram_tensor("dbg_" + name, shape, dt, kind="ExternalOutput").ap()))
        return a

    def finish():
        P.barrier()
        for a, b in dbg_list:
            P.op("sp", lambda e, a=a, b=b: e.dma_start(out=b, in_=a), kind="dma")
        P.barrier()
        P.emit()
        return nc

    wgu_s = [dscr(f"wgu_s{i}", [NHC, 128, 8, 2, 128]) for i in range(2)]
    wd_s = [dscr(f"wd_s{i}", [NHC, 128, D]) for i in range(2)]
    win_s = dscr("win_s", [7, 128, 8, 512])
    wout_s = dscr("wout_s", [2, 128, 8, 512])
    wq_s = dscr("wq_s", [2, 128, 8, 512])
    wkv_s = dscr("wkv_s", [4, 128, 8, 512])
    wo_s = dscr("wo_s", [2, 128, 8, 512])
    x1_s = dscr("x1_s", [T, D], F32)
    big = dscr("big", [NCH, 3 * 128, PKW])
    ain = dscr("ain", [NCH, 128, PKW])
    seq = dscr("seq", [2, NCH, 128, PKW])
    msb_my = dscr("msb_my", [512, T]); mhg_my = dscr("mhg_my", [2, T, 256])
    mhg_in = dscr("mhg_in", [2, T, 256]); mhg_out = dscr("mhg_out", [2, 2 * T, 256])
    msb_in = dscr("msb_in", [2, 256, T]); msb_out = dscr("msb_out", [2, 512, T])
    RG = [[0, 1], [2, 3], [4, 5], [6, 7]]

    es = contextlib.ExitStack()
    with es:
        ARENA_E = 94 * 1024
        arena_t = es.enter_context(nc.sbuf_tensor("arena", [128, ARENA_E], BF16))
        A = Arena(arena_t, ARENA_E)
        cb = es.enter_context(nc.sbuf_tensor("cb_sb", [128, 776], BF16))
        cf = es.enter_context(nc.sbuf_tensor("cf_sb", [128, 132], F32))
        ident = cb[:, 0:128]; tri = cb[:, 128:256]; omt = cb[:, 256:384]
        sbmask = cb[:, 384:512]; causal = cb[:, 512:640]
        Mm = cb[:, 640:768]; ind3 = cb[:, 768:772]
        ones_bf = es.enter_context(nc.sbuf_tensor("ones_bf", [128, 128], BF16))
        ps = [es.enter_context(nc.psum_tensor(f"ps{i}", [128, 512], F32)) for i in range(8)]

        def dma(eng, out, in_, r=(), w=(), **kw):
            return P.op(eng, lambda e, out=out, in_=in_, kw=kw: e.dma_start(out=out, in_=in_, **kw), r=r, w=w, kind="dma")

        def mm(out, lhsT, rhs, start, stop, r=(), w=()):
            return P.op("pe", lambda e: e.matmul(out, lhsT, rhs, start=start, stop=stop), r=r, w=w)

        def tp(out, in_, r=(), w=()):
            return P.op("pe", lambda e: e.transpose(out, in_, ident), r=list(r) + ["cb"], w=w)

        def act(out, in_, func, r=(), w=(), **kw):
            return P.op("act", lambda e: e.activation(out, in_, func, **kw), r=r, w=w)

        def vop(eng, name, *args, r=(), w=(), **kw):
            return P.op(eng, lambda e: getattr(e, name)(*args, **kw), r=r, w=w)

        dma("sp", cb[:], cb_d, w=["cb"])
        dma("sp", cf[:], cf_d, w=["cf"])
        vop("pool", "memset", ones_bf[:], 1.0, w=["ones"])

        def prep_w(src, dst, ngrp, gw, name):
            for kc in range(8):
                dma("pool", dst[:, :, kc, :], src[kc * 128:(kc + 1) * 128, :].rearrange("p (g c) -> g p c", c=gw),
                    w=[(name, kc)])

        def prep_ffn(i):
            for kc in range(8):
                for gu in range(2):
                    dma("pool", wgu_s[i][:, :, kc, gu, :],
                        wgu_d[i][kc * 128:(kc + 1) * 128, gu * DFF:(gu + 1) * DFF].rearrange("p (h c) -> h p c", c=128),
                        w=[(f"wgu{i}", kc, gu)])
            for q4 in range(2):
                dma("pool", wd_s[i][q4 * 11:(q4 + 1) * 11],
                    wd_d[i][q4 * 11 * 128:(q4 + 1) * 11 * 128, :].rearrange("(h p) c -> h p c", p=128), w=[(f"wd{i}", q4)])

        def wgu_keys(i):
            return [(f"wgu{i}", kc, gu) for kc in range(8) for gu in range(2)]

        def wd_keys(i):
            return [(f"wd{i}", q4) for q4 in range(2)]

        def wkeys(name):
            return [(name, kc) for kc in range(8)]

        prep_ffn(0)
        prep_w(win_d, win_s, 7, 512, "win")
        prep_w(wout_d, wout_s, 2, 512, "wout")
        prep_w(wq_d, wq_s, 2, 512, "wq")
        prep_w(wkv_d, wkv_s, 4, 512, "wkv")
        prep_w(wo_d, wo_s, 2, 512, "wo")
        prep_ffn(1)

        if stop == "p0":
            return finish()
        def alloc_common():
            t = {}
            t["xres"] = A.alloc([128, 4, D], F32)
            t["xn"] = A.alloc([128, 4, D], BF16)
            t["junk"] = A.alloc([128, D], BF16)
            t["xnT"] = A.alloc([128, 8, CH], BF16)
            t["hT"] = A.alloc([128, NHC, CH], BF16)
            t["wgu"] = [A.alloc([128, 8, 2, 128], BF16) for _ in range(3)]
            t["wd"] = A.alloc([128, NHC, D], BF16)
            t["wg"] = [A.alloc([128, 8, 512], BF16) for _ in range(2)]
            t["gain"] = [A.alloc([128, D], F32) for _ in range(3)]
            t["ss"] = A.alloc([128, 8], F32)
            t["rstd"] = A.alloc([128, 8], F32)
            t["mhalf"] = A.alloc([128, 8], F32)
            t["sg"] = [A.alloc([128, CH], BF16) for _ in range(2)]
            return t

        state = {"wgu_i": 0, "wg_i": 0}

        def load_gain(t, slot, row):
            dma("sp", t["gain"][slot], gains_d[row:row + 1, :].to_broadcast([128, D]), w=[("gain", slot)])

        def rms_T(t, gslot, src="xres", ntt=4, dstT="xnT"):
            xres = t[src]
            for tt in range(ntt):
                act(t["junk"], xres[:, tt, :], AF.Square, r=[(src, tt)], w=["junk", ("ss", tt)], accum_out=t["ss"][:, tt:tt + 1])
            vop("pool", "tensor_scalar", t["rstd"][:, 0:ntt], t["ss"][:, 0:ntt], 1.0 / D, EPS, ALU.mult, ALU.add,
                r=[("ss", tt) for tt in range(ntt)], w=["rstd_a"])
            vop("pool", "tensor_tensor", t["rstd"][:, 0:ntt], t["rstd"][:, 0:ntt], t["mhalf"][:, 0:ntt], ALU.pow,
                r=["rstd_a", "mhalf"], w=["rstd"])
            for tt in range(ntt):
                vop("dve", "scalar_tensor_tensor", t["xn"][:, tt, :], xres[:, tt, :], t["rstd"][:, tt:tt + 1],
                    t["gain"][gslot], ALU.mult, ALU.mult, r=[(src, tt), "rstd", ("gain", gslot)], w=[("xn", tt)])
            for tt in range(ntt):
                for half in range(2):
                    pst = ps[6 + half][:, 0:256].bitcast(BF16).rearrange("p (a b) -> p a b", a=4)
                    for j in range(4):
                        kc = half * 4 + j
                        tp(pst[:, j, :], t["xn"][:, tt, kc * 128:(kc + 1) * 128], r=[("xn", tt)], w=[("ps", 6 + half)])
                    eng = "act" if half == 0 else "dve"
                    if eng == "act":
                        act(t[dstT][:, half * 4:half * 4 + 4, tt * 128:(tt + 1) * 128], pst, AF.Copy,
                            r=[("ps", 6 + half)], w=[(dstT, tt)])
                    else:
                        vop("dve", "tensor_copy", t[dstT][:, half * 4:half * 4 + 4, tt * 128:(tt + 1) * 128], pst,
                            r=[("ps", 6 + half)], w=[(dstT, tt)])

        def ffn(t, i, ci):
            for q4 in range(2):
                dma("sp", t["wd"][:, q4 * 11:(q4 + 1) * 11, :], wd_s[i][q4 * 11:(q4 + 1) * 11].rearrange("h p c -> p h c"),
                    r=wd_keys(i), w=[("wdt", q4)])
            xT = [("xnT", tt) for tt in range(4)]
            for hc in range(NHC):
                wi = state["wgu_i"] % 3; state["wgu_i"] += 1
                wt = t["wgu"][wi]
                dma("sp", wt, wgu_s[i][hc], r=wgu_keys(i), w=[("wgut", wi)])
                pg = ps[(hc % 2) * 2]; pu = ps[(hc % 2) * 2 + 1]
                for gu, pp in ((0, pg), (1, pu)):
                    for kc in range(8):
                        mm(pp[:], wt[:, kc, gu, :], t["xnT"][:, kc, :], kc == 0, kc == 7,
                           r=[("wgut", wi)] + xT, w=[("ps", (hc % 2) * 2 + gu)])
                sg = t["sg"][hc % 2]
                act(sg, pg[:], AF.Silu, r=[("ps", (hc % 2) * 2)], w=[("sg", hc % 2)])
                vop("dve", "tensor_tensor", t["hT"][:, hc, :], sg, pu[:], ALU.mult,
                    r=[("sg", hc % 2), ("ps", (hc % 2) * 2 + 1)], w=[("hT", hc)])
            hk = [("hT", hc) for hc in range(NHC)]
            n = 0
            for tt in range(4):
                for half in range(2):
                    bk = 4 + n % 2; pp = ps[bk]; n += 1
                    for hc in range(NHC):
                        mm(pp[:], t["hT"][:, hc, tt * 128:(tt + 1) * 128], t["wd"][:, hc, half * 512:(half + 1) * 512],
                           hc == 0, hc == NHC - 1, r=[("hT", hc), ("wdt", hc // 11)], w=[("ps", bk)])
                    vop("dve", "scalar_tensor_tensor", t["xres"][:, tt, half * 512:(half + 1) * 512], pp[:], 0.5,
                        t["xres"][:, tt, half * 512:(half + 1) * 512], ALU.mult, ALU.add,
                        r=[("ps", bk), ("xres", tt)], w=[("xres", tt)])

        def load_wg(t, src, g, keys):
            wi = state["wg_i"] % 2; state["wg_i"] += 1
            dma("sp", t["wg"][wi], src[g], r=keys, w=[("wgt", wi)])
            return wi

        A.reset()
        t = alloc_common()
        pk = [A.alloc([128, PKW], BF16) for _ in range(2)]
        vop("pool", "memset", t["mhalf"], -0.5, w=["mhalf"])
        load_gain(t, 0, 0); load_gain(t, 1, 1)
        par_holder = {}

        def sp_par(e):
            if "v" not in par_holder:
                par_holder["v"] = e.partition_id() % 2
            return par_holder["v"]

        for ci in range(NCH):
            for tt in range(4):
                dma("sp", t["xres"][:, tt, :], x_d[ci * CH + tt * 128: ci * CH + (tt + 1) * 128, :], w=[("xres", tt)])
            if stop == "p1x":
                return finish()
            rms_T(t, 0)
            if stop == "p1a":
                return finish()
            ffn(t, 0, ci)
            if stop == "p1b":
                return finish()
            for tt in range(4):
                dma("sp", x1_s[ci * CH + tt * 128: ci * CH + (tt + 1) * 128, :], t["xres"][:, tt, :], r=[("xres", tt)],
                    w=[("x1s", ci, tt)])
            if stop == "p1b1":
                return finish()
            rms_T(t, 1)
            if stop == "p1b2":
                return finish()
            xT = [("xnT", tt) for tt in range(4)]
            for g in range(2):
                wi = load_wg(t, win_s, g, wkeys("win"))
                for i4 in range(4):
                    pp = ps[4 + i4 % 2]
                    for kc in range(8):
                        mm(pp[:], t["wg"][wi][:, kc, i4 * 128:(i4 + 1) * 128], t["xnT"][:, kc, :], kc == 0, kc == 7,
                           r=[("wgt", wi)] + xT, w=[("ps", 4 + i4 % 2)])
                    if i4 % 2 == 0:
                        act(pk[g][:, i4 * 512:(i4 + 1) * 512], pp[:], AF.Copy, r=[("ps", 4 + i4 % 2)], w=[("pk", g)])
                    else:
                        vop("dve", "tensor_copy", pk[g][:, i4 * 512:(i4 + 1) * 512], pp[:], r=[("ps", 4 + i4 % 2)], w=[("pk", g)])
            if stop == "p1b3":
                return finish()
            for g in range(2, 7):
                if stop == f"p1g{g}":
                    return finish()
                wi = load_wg(t, win_s, g, wkeys("win"))
                for tt in range(4):
                    pp = ps[4 + tt % 2]
                    for kc in range(8):
                        mm(pp[:], t["xnT"][:, kc, tt * 128:(tt + 1) * 128], t["wg"][wi][:, kc, :], kc == 0, kc == 7,
                           r=[("wgt", wi), ("xnT", tt)], w=[("ps", 4 + tt % 2)])
                    if g == 4:
                        for m in range(2):
                            dst = pk[m][:, 6144 + tt * 512: 6144 + (tt + 1) * 512].bitcast(F32)
                            if tt % 2 == 0:
                                act(dst, pp[:, m * 256:(m + 1) * 256], AF.Copy, r=[("ps", 4 + tt % 2)], w=[("pk", m)])
                            else:
                                vop("dve", "tensor_copy", dst, pp[:, m * 256:(m + 1) * 256], r=[("ps", 4 + tt % 2)], w=[("pk", m)])
                    else:
                        m = 0 if g < 4 else 1
                        base = 2048 if g in (2, 5) else 4096
                        dstv = pk[m][:, base:base + 2048].rearrange("p (s a c) -> p s a c", s=2, a=4)[:, :, tt, :]
                        src = pp[:].rearrange("p (s c) -> p s c", s=2)
                        if tt % 2 == 0:
                            act(dstv, src, AF.Copy, r=[("ps", 4 + tt % 2)], w=[("pk", m)])
                        else:
                            vop("dve", "tensor_copy", dstv, src, r=[("ps", 4 + tt % 2)], w=[("pk", m)])
            if stop == "p1c":
                return finish()
            dma("sp", big[ci, 256:384, :], pk[0], r=[("pk", 0)], w=[("big", ci, 2)])
            dma("sp", ain[ci], pk[1], r=[("pk", 1)], w=[("ain", ci)])
            if stop == "p1d":
                return finish()
            P.op("pool", lambda e, ci=ci: e.collective_compute("AllGather", ALU.bypass, replica_groups=RG,
                                                               ins=[ain[ci]], outs=[big[ci, 0:256, :]]),
                 r=[("ain", ci)], w=[("big", ci, 0), ("big", ci, 1)], kind="cc")

        P.barrier()
        if stop == "p1":
            return finish()
        if debug:
            early = nc.dram_tensor("dbg_x1_early", [T, D], F32, kind="ExternalOutput").ap()
            P.op("sp", lambda e: e.dma_start(out=early, in_=x1_s), kind="dma")
            P.barrier()

        A.reset()
        QK = A.alloc([128, 4, S], BF16)
        V = A.alloc([128, NT, 256], BF16)
        sbg = A.alloc([64, 4], F32)
        hg_mark = A.off
        hq = [A.alloc([128, 4, 256], BF16) for _ in range(2)]
        hi = [A.alloc([128, 4, 256], BF16) for _ in range(2)]
        hg = [A.alloc([128, 4, 256], BF16) for _ in range(2)]
        hfr = [A.alloc([128, 2048], BF16) for _ in range(2)]
        hf = [x.bitcast(F32).rearrange("p (a c) -> p a c", a=4) for x in hfr]
        lbr = A.alloc([128, 3, 256], F32)
        lb = A.alloc([128, 256], F32); oml = A.alloc([128, 256], F32); gnb = A.alloc([128, 256], F32)
        Sst = [A.alloc([128, 128], F32) for _ in range(2)]
        Stmp = [A.alloc([128, 128], F32) for _ in range(2)]
        Sbf = [A.alloc([128, 128], BF16) for _ in range(2)]

        def W2(n, shape=(128, 256), dt=F32):
            return [A.alloc(list(shape), dt) for _ in range(n)]
        lhi = W2(2, dt=BF16); llo = W2(2, dt=BF16); e1 = W2(2); f_ = W2(2); logf = W2(2); kk = W2(2); eq = W2(2); ek = W2(2); e2 = W2(2); sgg = W2(2)
        qh = W2(2, dt=BF16); kh = W2(2, dt=BF16)
        qhT = W2(2, (128, 2, 128), BF16); khT0 = W2(2, (128, 2, 128), BF16); khT1 = W2(2, (128, 2, 128), BF16)
        am = W2(2, (128, 2, 128), BF16)
        esc = W2(2, (128, 2, 3), F32)
        hss = W2(2, (128, 2), F32); hln = W2(2, (128, 2), F32); hrs = W2(2, (128, 2), F32)
        mst = W2(2, (128, 256), BF16)
        junk2 = A.alloc([128, 128], BF16)

        if stop == "p2a0":
            return finish()
        if stop == "p2a":
            return finish()

        big4 = big.rearrange("n (s p) c -> n s p c", s=3)
        for hfi in range(2):
            def fnq(e, hfi=hfi):
                par = sp_par(e)
                idx = ((1 - par) * 2) if hfi == 0 else (par + 1)
                return e.dma_start(out=seq[hfi], in_=big4[:, bass.ds(idx, 1), :, :].rearrange("n o p c -> (n o) p c"))
            P.op("sp", fnq, w=[("seq", hfi)], kind="dma")

        if stop == "p2b":
            return finish()

        def slot_ap(ci, hfi, c0, c1):
            return seq[hfi, ci][:, c0:c1]

        def dyn_dma(out, src, r=(), w=(), rearr=None, hfi=0):
            if rearr is not None:
                src = rearr(src)
            return dma("sp", out, src, r=list(r) + [("seq", hfi)], w=w)

        dma("sp", lbr.rearrange("p a c -> p (a c)"), hgp_d.rearrange("a c -> (a c)").rearrange("(o n) -> o n", o=1).to_broadcast([128, 768]), w=["lbr"])
        dma("sp", sbg, sbg_d, w=["sbg"])
        vop("dve", "tensor_tensor", lb, lbr[:, 1, :], lbr[:, 0, :], ALU.subtract, r=["lbr"], w=["lb0"])
        act(lb, lb, AF.Exp, r=["lb0"], w=["lb1"])
        vop("dve", "tensor_scalar", lb, lb, 1.0, None, ALU.add, r=["lb1"], w=["lb2"])
        vop("dve", "reciprocal", lb, lb, r=["lb2"], w=["lb"])
        vop("dve", "tensor_scalar", oml, lb, -1.0, 1.0, ALU.mult, ALU.add, r=["lb"], w=["oml"])
        vop("dve", "tensor_copy", gnb, lbr[:, 2, :], r=["lbr"], w=["gnb"])
        for h in range(2):
            vop("dve", "memset", Sst[h], 0.0, w=[("S", h)])
            for i in range(2):
                vop("pool", "memset", khT0[i][:, h, :], 0.0, w=[("khT0", i, h)])
                vop("pool", "memset", khT1[i][:, h, :], 0.0, w=[("khT1", i, h)])

        if stop == "p2c":
            return finish()
        for hfi in range(2):
            for ci in range(NCH):
                n0 = hfi * T + ci * CH
                deps = [("big", ci, 0), ("big", ci, 1), ("big", ci, 2)]
                dyn_dma(QK[:, :, n0:n0 + CH], slot_ap(ci, hfi, 0, 2048), w=[("QK", n0 // CH)],
                        rearr=lambda s: s.rearrange("p (i t) -> p i t", i=4), hfi=hfi)
                dyn_dma(V[:, n0 // 128:n0 // 128 + 4, :], slot_ap(ci, hfi, 2048, 3072), w=[("V", n0 // CH)],
                        rearr=lambda s: s.rearrange("p (a c) -> p a c", a=4), hfi=hfi)

        if stop == "p2l":
            return finish()
        mhg_v = mhg_in.rearrange("h (n p) c -> (h n) p c", p=128)
        for hfi in range(2):
            for ci in range(NCH):
                bi = (hfi * NCH + ci) % 2
                dyn_dma(hq[bi], slot_ap(ci, hfi, 3072, 4096), w=[("hq", bi)], rearr=lambda s: s.rearrange("p (a c) -> p a c", a=4), hfi=hfi)
                dyn_dma(hi[bi], slot_ap(ci, hfi, 4096, 5120), w=[("hi", bi)], rearr=lambda s: s.rearrange("p (a c) -> p a c", a=4), hfi=hfi)
                dyn_dma(hg[bi], slot_ap(ci, hfi, 5120, 6144), w=[("hg", bi)], rearr=lambda s: s.rearrange("p (a c) -> p a c", a=4), hfi=hfi)
                dyn_dma(hfr[bi], slot_ap(ci, hfi, 6144, 8192), w=[("hf", bi)], hfi=hfi)
                if stop == "hgA0":
                    return finish()
                for tt in range(4):
                    gt = (hfi * T + ci * CH) // 128 + tt
                    w_ = gt % 2
                    K = lambda name: (name, w_)
                    act(e1[w_], hf[bi][:, tt, :], AF.Exp, r=[("hf", bi)], w=[K("e1")], scale=-1.0)
                    if stop == "hgA1":
                        return finish()
                    vop("dve", "tensor_scalar", e1[w_], e1[w_], 1.0, None, ALU.add, r=[K("e1")], w=[K("e1")])
                    if stop == "hgA2":
                        return finish()
                    vop("dve", "reciprocal", e1[w_], e1[w_], r=[K("e1")], w=[K("e1")])
                    if stop == "hgA3":
                        return finish()
                    vop("dve", "tensor_tensor", f_[w_], e1[w_], oml, ALU.mult, r=[K("e1"), "oml"], w=[K("f")])
                    vop("dve", "tensor_tensor", f_[w_], f_[w_], lb, ALU.add, r=[K("f"), "lb"], w=[K("f")])
                    act(logf[w_], f_[w_], AF.Ln, r=[K("f")], w=[K("logf")])
                    vop("dve", "tensor_scalar", kk[w_], f_[w_], -1.0, 1.0, ALU.mult, ALU.add, r=[K("f")], w=[K("kk")])
                    if stop == "hgA":
                        return finish()
                    vop("pool", "tensor_copy", lhi[w_], logf[w_], r=[K("logf")], w=[K("lhi")])
                    vop("pool", "tensor_tensor", llo[w_], logf[w_], lhi[w_], ALU.subtract, r=[K("logf"), K("lhi")], w=[K("llo")])
                    mm(ps[0][:, 0:256], Mm, lhi[w_], True, False, r=["cb", K("lhi")], w=["p_c1"])
                    mm(ps[0][:, 0:256], Mm, llo[w_], False, True, r=["cb", K("llo")], w=["p_c1"])
                    for h in range(2):
                        mm(ps[1][:, h * 4:h * 4 + 4], lhi[w_][:, h * 128:(h + 1) * 128], ind3, True, False,
                           r=["cb", K("lhi")], w=["p_sc"])
                        mm(ps[1][:, h * 4:h * 4 + 4], llo[w_][:, h * 128:(h + 1) * 128], ind3, False, True,
                           r=["cb", K("llo")], w=["p_sc"])
                    act(eq[w_], ps[0][:, 0:256], AF.Exp, r=["p_c1"], w=[K("eq")])
                    act(ek[w_], ps[0][:, 0:256], AF.Exp, r=["p_c1"], w=[K("ek")], scale=-1.0)
                    act(esc[w_], ps[1][:, 0:8].rearrange("p (h c) -> p h c", h=2)[:, :, 0:3], AF.Exp, r=["p_sc"], w=[K("esc")])
                    vop("dve", "tensor_tensor", qh[w_], hq[bi][:, tt, :], eq[w_], ALU.mult, r=[("hq", bi), K("eq")], w=[K("qh")])
                    vop("dve", "tensor_tensor", kh[w_], kk[w_], ek[w_], ALU.mult, r=[K("kk"), K("ek")], w=[K("kh")])
                    if stop == "hgB":
                        return finish()
                    act(e2[w_], hg[bi][:, tt, :], AF.Exp, r=[("hg", bi)], w=[K("e2")], scale=-1.0)
                    vop("dve", "tensor_scalar", e2[w_], e2[w_], 1.0, None, ALU.add, r=[K("e2")], w=[K("e2")])
                    vop("dve", "reciprocal", e2[w_], e2[w_], r=[K("e2")], w=[K("e2")])
                    vop("dve", "tensor_tensor", sgg[w_], e2[w_], hg[bi][:, tt, :], ALU.mult, r=[K("e2"), ("hg", bi)], w=[K("sgg")])
                    vop("dve", "tensor_tensor", sgg[w_], sgg[w_], gnb, ALU.mult, r=[K("sgg"), "gnb"], w=[K("sgg")])
                    ptq = ps[2][:, 0:128].bitcast(BF16).rearrange("p (h c) -> p h c", h=2)
                    ptk = ps[3][:, 0:128].bitcast(BF16).rearrange("p (h c) -> p h c", h=2)
                    for h in range(2):
                        tp(ptq[:, h, :], qh[w_][:, h * 128:(h + 1) * 128], r=[K("qh")], w=["ptq"])
                        tp(ptk[:, h, :], kh[w_][:, h * 128:(h + 1) * 128], r=[K("kh")], w=["ptk"])
                    act(qhT[w_], ptq, AF.Copy, r=["ptq"], w=[K("qhT")])
                    vop("dve", "tensor_copy", khT0[w_][:, :, 0:64], ptk[:, :, 0:64], r=["ptk"], w=[K("khT0")])
                    vop("dve", "tensor_copy", khT1[w_][:, :, 64:128], ptk[:, :, 64:128], r=["ptk"], w=[K("khT1")])
                    if stop == "hgC":
                        return finish()
                    for h in range(2):
                        pa = ps[4][:, h * 128:(h + 1) * 128]
                        mm(pa, khT0[w_][:, h, :], qhT[w_][:, h, :], True, False, r=[K("khT0"), K("qhT")], w=[("p_a", h)])
                        mm(pa[:, 64:128], khT1[w_][:, h, :], qhT[w_][:, h, 64:128], False, True, r=[K("khT1"), K("qhT")], w=[("p_a", h)])
                        vop("dve", "tensor_tensor", am[w_][:, h, :], pa, causal, ALU.mult, r=[("p_a", h), "cb"], w=[K("am") + (h,)])
                        act(Sbf[h], Sst[h], AF.Copy, r=[("S", h), K("esc")], w=[("Sbf", h)], scale=esc[w_][:, h, 0:1])
                        vop("dve", "tensor_scalar", Stmp[h], Sst[h], esc[w_][:, h, 1:2], None, ALU.mult,
                            r=[("S", h), K("esc")], w=[("Stmp", h)])
                        po = ps[5][:, h * 128:(h + 1) * 128]
                        ih = hi[bi][:, tt, h * 128:(h + 1) * 128]
                        mm(po, am[w_][:, h, :], ih, True, False, r=[K("am") + (h,), ("hi", bi)], w=[("p_o", h)])
                        mm(po, qhT[w_][:, h, :], Sbf[h], False, True, r=[K("qhT"), ("Sbf", h)], w=[("p_o", h)])
                        pu = ps[6][:, h * 128:(h + 1) * 128]
                        mm(pu, kh[w_][:, h * 128:(h + 1) * 128], ih, True, True, r=[K("kh"), ("hi", bi)], w=[("p_u", h)])
                        vop("dve", "scalar_tensor_tensor", Sst[h], pu, esc[w_][:, h, 2:3], Stmp[h], ALU.mult, ALU.add,
                            r=[("p_u", h), K("esc"), ("Stmp", h)], w=[("S", h)])
                        act(junk2, po, AF.Square, r=[("p_o", h)], w=["junk2", K("hss") + (h,)], accum_out=hss[w_][:, h:h + 1])
                    if stop == "hgD":
                        return finish()
                    act(hln[w_], hss[w_], AF.Ln, r=[K("hss") + (0,), K("hss") + (1,)], w=[K("hln")], scale=1.0 / 128, bias=EPS)
                    act(hrs[w_], hln[w_], AF.Exp, r=[K("hln")], w=[K("hrs")], scale=-0.5)
                    for h in range(2):
                        po = ps[5][:, h * 128:(h + 1) * 128]
                        vop("dve", "scalar_tensor_tensor", mst[w_][:, h * 128:(h + 1) * 128], po, hrs[w_][:, h:h + 1],
                            sgg[w_][:, h * 128:(h + 1) * 128], ALU.mult, ALU.mult,
                            r=[("p_o", h), K("hrs"), K("sgg")], w=[K("mst")])
                    dma("sp", mhg_v[gt], mst[w_], r=[K("mst")], w=[("mhg", gt)])

        for hh in range(2):
            P.op("pool", lambda e, hh=hh: e.collective_compute("AllGather", ALU.bypass, replica_groups=RG,
                                                               ins=[mhg_in[hh]], outs=[mhg_out[hh]]),
                 r=[("mhg", gt) for gt in range(NT)], w=[("mhg_out", hh)], kind="cc")

        P.barrier()
        if stop == "p2hg":
            return finish()
        A.off = hg_mark
        KBB = 4
        E_ = [[W2(KBB, (128, 512), F32) for _ in range(2)] for _ in range(2)]; L_ = [[W2(KBB, (128, 512), BF16) for _ in range(2)] for _ in range(2)]; X_ = W2(2, (128, 512), F32); Wt = W2(2, (128, 512), BF16)
        sq_ = W2(2, (64, 512), BF16); rl_ = W2(2, (64, 512), F32); rs_ = W2(2, (64, 512), F32); mo_ = W2(2, (64, 512), BF16)
        NG = S // 512
        for hp in range(2):
            for G in range(NG):
                t0 = G * 512
                streams = [0, 1]
                kbs = list(range(4 * G + 3, -1, -1))
                batches = [kbs[b0:b0 + KBB] for b0 in range(0, len(kbs), KBB)]

                def stageA(bi, sl):
                    kb = batches[bi][sl]; bb = bi % 2
                    di = kb - 4 * G
                    c0 = max(0, di) * 128
                    kq = [("QK", kb // 4), ("QK", G)]
                    for st in streams:
                        r0 = st * 64
                        zb = (0 if st == 0 else 3) if sl % 2 == 0 else 6 + st
                        zp = ps[zb]
                        mm(zp[:, c0:512], QK[r0:r0 + 64, 2 + hp, kb * 128:(kb + 1) * 128], QK[r0:r0 + 64, hp, t0 + c0:t0 + 512],
                           True, True, r=kq, w=[("psb", zb)])
                        act(E_[st][bb][sl][:, c0:512], zp[:, c0:512], AF.Exp, r=[("psb", zb)], w=[("E", st, bb, sl)], scale=0.125)
                        if di >= 0:
                            vop("pool", "tensor_tensor", E_[st][bb][sl][:, c0:c0 + 128], E_[st][bb][sl][:, c0:c0 + 128], sbmask, ALU.mult,
                                r=[("E", st, bb, sl), "cb"], w=[("E", st, bb, sl)])

                def stageB(bi):
                    bb = bi % 2
                    for sl, kb in enumerate(batches[bi]):
                        c0 = max(0, kb - 4 * G) * 128
                        for st in streams:
                            act(L_[st][bb][sl][:, c0:512], E_[st][bb][sl][:, c0:512], AF.Ln, r=[("E", st, bb, sl)], w=[("L", st, bb, sl)], bias=1.0)

                def stageC(bi, sl):
                    kb = batches[bi][sl]; bb = bi % 2
                    c0 = max(0, kb - 4 * G) * 128
                    first = kb == 4 * G + 3
                    vk = [("V", kb // 4)]
                    for st in streams:
                        pp = ps[st * 3 + 1]
                        mm(pp[:, c0:512], tri, L_[st][bb][sl][:, c0:512], first, True, r=["cb", ("L", st, bb, sl)], w=[("pp", st)])
                    for st in streams:
                        pp = ps[st * 3 + 1]
                        act(X_[st][:, c0:512], pp[:, c0:512], AF.Exp, r=[("pp", st)], w=[("X", st)], scale=-1.0)
                        mm(pp[:, c0:512], omt, L_[st][bb][sl][:, c0:512], False, True, r=["cb", ("L", st, bb, sl), ("X", st)], w=[("pp", st)])
                        vop("dve", "tensor_tensor", Wt[st][:, c0:512], E_[st][bb][sl][:, c0:512], X_[st][:, c0:512], ALU.mult,
                            r=[("E", st, bb, sl), ("X", st)], w=[("W", st)])
                    for st in streams:
                        head = hp * 2 + st
                        op_ = ps[st * 3 + 2]
                        vv = V[:, kb, head * 64:(head + 1) * 64]
                        mm(op_[0:64, c0:512], vv, Wt[st][:, c0:512], first, True, r=vk + [("W", st)], w=[("op", st)])

                for sl in range(len(batches[0])):
                    stageA(0, sl)
                stageB(0)
                for bi in range(len(batches)):
                    nxt = bi + 1 < len(batches)
                    for sl in range(len(batches[bi])):
                        if nxt:
                            stageA(bi + 1, sl)
                        stageC(bi, sl)
                    if nxt:
                        stageB(bi + 1)
                if stop == "sbD":
                    return finish()
                for h2 in streams:
                    st = h2; head = hp * 2 + h2
                    op_ = ps[st * 3 + 2]; pn = ps[6 + st]
                    act(sq_[st], op_[0:64, :], AF.Square, r=[("op", st)], w=[("sq", st)])
                    mm(pn[0:64, :], ones_bf[0:64, 0:64], sq_[st], True, True, r=["ones", ("sq", st)], w=[("psb", 6 + st)])
                    act(rl_[st], pn[0:64, :], AF.Ln, r=[("psb", 6 + st)], w=[("rl", st)], scale=1.0 / 64, bias=EPS)
                    act(rs_[st], rl_[st], AF.Exp, r=[("rl", st)], w=[("rs", st)], scale=-0.5)
                    vop("dve", "scalar_tensor_tensor", mo_[st], op_[0:64, :], sbg[:, head:head + 1], rs_[st], ALU.mult, ALU.mult,
                        r=[("op", st), "sbg", ("rs", st)], w=[("mo", st)])
                    dma("sp", msb_in[t0 // T, head * 64:(head + 1) * 64, t0 % T:t0 % T + 512], mo_[st], r=[("mo", st)], w=[("msb", head, G)])

        if stop == "sbE":
            return finish()
        for hh in range(2):
            P.op("pool", lambda e, hh=hh: e.collective_compute("AllGather", ALU.bypass, replica_groups=RG,
                                                               ins=[msb_in[hh]], outs=[msb_out[hh]]),
                 r=[("msb", h, G) for h in range(4) for G in range(NG)], w=[("msb_out", hh)], kind="cc")

        P.barrier()
        if stop == "p2":
            return finish()

        A.reset()
        t = alloc_common()
        mixT = A.alloc([128, 8, CH], BF16)
        hgm = A.alloc([128, 4, 2, 256], BF16)
        memx = t["xres"]
        memT = A.alloc([128, 8, NMEM], BF16)
        kmT = A.alloc([128, 8, NMEM], BF16)
        vm = A.alloc([128, 2, D], BF16)
        qT = A.alloc([128, 8, CH], BF16)
        pb = A.alloc([128, 4, NMEM], BF16)
        pT = A.alloc([128, 8, 128], BF16)
        ob = A.alloc([128, D], BF16)
        oT = t["xnT"]
        mx = A.alloc([128, 4], F32); nb = A.alloc([128, 4], F32); rsum = A.alloc([128, 4], F32); rinv = A.alloc([128, 4], F32)
        yout = A.alloc([128, D], F32)
        vop("pool", "memset", t["mhalf"], -0.5, w=["mhalf"])

        load_gain(t, 0, 3)
        for tt in range(2):
            dma("sp", memx[:, tt, :], mem_d[tt * 128:(tt + 1) * 128, :], w=[("xres", tt)])
        t["memT"] = memT
        rms_T(t, 0, src="xres", ntt=2, dstT="memT")
        mT = [("memT", tt) for tt in range(2)]
        for g in range(2):
            wi = load_wg(t, wkv_s, g, wkeys("wkv"))
            for i4 in range(4):
                pp = ps[4 + i4 % 2]
                for kc in range(8):
                    mm(pp[:, 0:NMEM], t["wg"][wi][:, kc, i4 * 128:(i4 + 1) * 128], memT[:, kc, :], kc == 0, kc == 7,
                       r=[("wgt", wi)] + mT, w=[("ps", 4 + i4 % 2)])
                act(kmT[:, g * 4 + i4, :], pp[:, 0:NMEM], AF.Copy, r=[("ps", 4 + i4 % 2)], w=["kmT"])
        for g in range(2):
            wi = load_wg(t, wkv_s, 2 + g, wkeys("wkv"))
            for tt in range(2):
                pp = ps[4 + tt % 2]
                for kc in range(8):
                    mm(pp[:], memT[:, kc, tt * 128:(tt + 1) * 128], t["wg"][wi][:, kc, :], kc == 0, kc == 7,
                       r=[("wgt", wi)] + mT, w=[("ps", 4 + tt % 2)])
                act(vm[:, tt, g * 512:(g + 1) * 512], pp[:], AF.Copy, r=[("ps", 4 + tt % 2)], w=["vm"])
        load_gain(t, 0, 2)
        load_gain(t, 1, 4)
        load_gain(t, 2, 5)

        def fn_msb(e):
            return e.dma_start(out=msb_my.rearrange("(o r) t -> o r t", o=1), in_=msb_out[bass.ds(sp_par(e), 1), :, :])
        P.op("sp", fn_msb, w=["msb_my"], kind="dma")

        def fn_mhg(e):
            return e.dma_start(out=mhg_my.rearrange("(o m) t c -> o (m t) c", o=1), in_=mhg_out[bass.ds(sp_par(e), 1), :, :])
        P.op("sp", fn_mhg, w=["mhg_my"], kind="dma")

        for ci in range(NCH):
            for tt in range(4):
                dma("sp", t["xres"][:, tt, :], x1_s[ci * CH + tt * 128: ci * CH + (tt + 1) * 128, :],
                    r=[("x1s", ci, tt)], w=[("xres", tt)])
            for m in range(2):
                for j in range(2):
                    dma("sp", mixT[:, 4 + 2 * m + j, :], msb_my[m * 256 + j * 128: m * 256 + (j + 1) * 128, ci * CH:(ci + 1) * CH],
                        r=["msb_my"], w=[("mixT_sb", m, j)])
                dma("sp", hgm[:, :, m, :], mhg_my[m, ci * CH:(ci + 1) * CH, :].rearrange("(a p) c -> p a c", p=128),
                    r=["mhg_my"], w=[("hgm", m)])
            for tt in range(4):
                pst = ps[6 + tt % 2][:, 0:256].bitcast(BF16).rearrange("p (a b) -> p a b", a=4)
                for m in range(2):
                    for h in range(2):
                        tp(pst[:, 2 * m + h, :], hgm[:, tt, m, h * 128:(h + 1) * 128], r=[("hgm", m)], w=[("ps", 6 + tt % 2)])
                if tt % 2 == 0:
                    act(mixT[:, 0:4, tt * 128:(tt + 1) * 128], pst, AF.Copy, r=[("ps", 6 + tt % 2)], w=[("mixT_hg", tt)])
                else:
                    vop("dve", "tensor_copy", mixT[:, 0:4, tt * 128:(tt + 1) * 128], pst, r=[("ps", 6 + tt % 2)], w=[("mixT_hg", tt)])
            mixk = [("mixT_sb", m, j) for m in range(2) for j in range(2)]

            def proj_resid(srcT, srckeys, wsrc, wname):
                for half in range(2):
                    wi = load_wg(t, wsrc, half, wkeys(wname))
                    for tt in range(4):
                        pp = ps[4 + tt % 2]
                        for kc in range(8):
                            mm(pp[:], srcT[:, kc, tt * 128:(tt + 1) * 128], t["wg"][wi][:, kc, :], kc == 0, kc == 7,
                               r=[("wgt", wi)] + srckeys(tt), w=[("ps", 4 + tt % 2)])
                        vop("dve", "tensor_tensor", t["xres"][:, tt, half * 512:(half + 1) * 512], pp[:],
                            t["xres"][:, tt, half * 512:(half + 1) * 512], ALU.add, r=[("ps", 4 + tt % 2), ("xres", tt)], w=[("xres", tt)])

            proj_resid(mixT, lambda tt: mixk + [("mixT_hg", tt)], wout_s, "wout")
            rms_T(t, 0)
            xT = [("xnT", tt) for tt in range(4)]
            for g in range(2):
                wi = load_wg(t, wq_s, g, wkeys("wq"))
                for i4 in range(4):
                    pp = ps[4 + i4 % 2]
                    for kc in range(8):
                        mm(pp[:], t["wg"][wi][:, kc, i4 * 128:(i4 + 1) * 128], t["xnT"][:, kc, :], kc == 0, kc == 7,
                           r=[("wgt", wi)] + xT, w=[("ps", 4 + i4 % 2)])
                    if i4 % 2 == 0:
                        act(qT[:, g * 4 + i4, :], pp[:], AF.Copy, r=[("ps", 4 + i4 % 2)], w=[("qT", g * 4 + i4)])
                    else:
                        vop("dve", "tensor_copy", qT[:, g * 4 + i4, :], pp[:], r=[("ps", 4 + i4 % 2)], w=[("qT", g * 4 + i4)])
            for tt in range(4):
                psc = [ps[0], ps[1]]
                for h in range(4):
                    dst = psc[h // 2][:, (h % 2) * 256:(h % 2 + 1) * 256]
                    for j in range(2):
                        mm(dst, qT[:, 2 * h + j, tt * 128:(tt + 1) * 128], kmT[:, 2 * h + j, :], j == 0, j == 1,
                           r=[("qT", 2 * h + j), "kmT"], w=[("ps", h // 2)])
                for b2 in range(2):
                    vop("dve", "tensor_reduce", mx[:, 2 * b2:2 * b2 + 2], psc[b2][:].rearrange("p (h n) -> p h n", h=2),
                        AX.X, ALU.max, r=[("ps", b2)], w=[("mx", b2)])
                vop("dve", "tensor_scalar", nb, mx, -1.0 / 16, None, ALU.mult, r=[("mx", 0), ("mx", 1)], w=["nb"])
                for h in range(4):
                    src = psc[h // 2][:, (h % 2) * 256:(h % 2 + 1) * 256]
                    act(pb[:, h, :], src, AF.Exp, r=[("ps", h // 2), "nb"], w=[("pb", h)], scale=1.0 / 16,
                        bias=nb[:, h:h + 1], accum_out=rsum[:, h:h + 1])
                vop("dve", "reciprocal", rinv, rsum, r=[("pb", h) for h in range(4)], w=["rinv"])
                for half in range(2):
                    pst = ps[6 + half][:, 0:256].bitcast(BF16).rearrange("p (a b) -> p a b", a=4)
                    for j in range(4):
                        idx = half * 4 + j
                        tp(pst[:, j, :], pb[:, idx // 2, (idx % 2) * 128:(idx % 2 + 1) * 128], r=[("pb", idx // 2)], w=[("ps", 6 + half)])
                    if half == 0:
                        act(pT[:, 0:4, :], pst, AF.Copy, r=[("ps", 6 + half)], w=[("pT", half)])
                    else:
                        vop("dve", "tensor_copy", pT[:, 4:8, :], pst, r=[("ps", 6 + half)], w=[("pT", half)])
                pov = [ps[2], ps[3]]
                for h in range(4):
                    dst = pov[h // 2][:, (h % 2) * 256:(h % 2 + 1) * 256]
                    for j in range(2):
                        mm(dst, pT[:, 2 * h + j, :], vm[:, j, h * 256:(h + 1) * 256], j == 0, j == 1,
                           r=[("pT", h // 2), "vm"], w=[("ps", 2 + h // 2)])
                for h in range(4):
                    src = pov[h // 2][:, (h % 2) * 256:(h % 2 + 1) * 256]
                    if h // 2 == 0:
                        act(ob[:, h * 256:(h + 1) * 256], src, AF.Copy, r=[("ps", 2 + h // 2), "rinv"], w=[("ob", h)], scale=rinv[:, h:h + 1])
                    else:
                        vop("dve", "tensor_scalar", ob[:, h * 256:(h + 1) * 256], src, rinv[:, h:h + 1], None, ALU.mult,
                            r=[("ps", 2 + h // 2), "rinv"], w=[("ob", h)])
                for half in range(2):
                    pst = ps[6 + half][:, 0:256].bitcast(BF16).rearrange("p (a b) -> p a b", a=4)
                    for j in range(4):
                        kc = half * 4 + j
                        tp(pst[:, j, :], ob[:, kc * 128:(kc + 1) * 128], r=[("ob", kc // 2)], w=[("ps", 6 + half)])
                    if half == 0:
                        act(oT[:, 0:4, tt * 128:(tt + 1) * 128], pst, AF.Copy, r=[("ps", 6 + half)], w=[("xnT", tt)])
                    else:
                        vop("dve", "tensor_copy", oT[:, 4:8, tt * 128:(tt + 1) * 128], pst, r=[("ps", 6 + half)], w=[("xnT", tt)])
            proj_resid(oT, lambda tt: [("xnT", tt)], wo_s, "wo")
            rms_T(t, 1)
            ffn(t, 1, ci)
            for tt in range(4):
                act(t["junk"], t["xres"][:, tt, :], AF.Square, r=[("xres", tt)], w=["junk", ("ss", tt)], accum_out=t["ss"][:, tt:tt + 1])
            vop("pool", "tensor_scalar", t["rstd"][:, 0:4], t["ss"][:, 0:4], 1.0 / D, EPS, ALU.mult, ALU.add,
                r=[("ss", tt) for tt in range(4)], w=["rstd_a"])
            vop("pool", "tensor_tensor", t["rstd"][:, 0:4], t["rstd"][:, 0:4], t["mhalf"][:, 0:4], ALU.pow,
                r=["rstd_a", "mhalf"], w=["rstd"])
            for tt in range(4):
                vop("dve", "scalar_tensor_tensor", yout, t["xres"][:, tt, :], t["rstd"][:, tt:tt + 1], t["gain"][2],
                    ALU.mult, ALU.mult, r=[("xres", tt), "rstd", ("gain", 2)], w=["yout"])
                dma("sp", out_d[ci * CH + tt * 128: ci * CH + (tt + 1) * 128, :], yout, r=["yout"], w=[("out", ci, tt)])

        return finish()


_CACHE = {}


def _consts():
    i = np.arange(128)
    ident = np.eye(128, dtype=np.float32)
    tri = (i[:, None] >= i[None, :]).astype(np.float32)
    omt = 1.0 - tri
    sbmask = (i[:, None] < i[None, :]).astype(np.float32)
    causal = (i[:, None] <= i[None, :]).astype(np.float32)
    Mm = (i[:, None] <= i[None, :]).astype(np.float32) - (i[:, None] <= 63).astype(np.float32)
    ind = np.stack([(i <= 63), np.ones(128, bool), (i >= 64)], axis=1).astype(np.float32)
    cb = np.concatenate([ident, tri, omt, sbmask, causal, Mm, ind, np.zeros((128, 5), np.float32)], axis=1).astype(ml_dtypes.bfloat16)
    cf = np.concatenate([Mm, ind, np.zeros((128, 1), np.float32)], axis=1).astype(np.float32)
    return cb, cf


def _win_perm(j):
    HGW = 512
    def hgcols(part, m):
        return list(range(part * HGW + m * 256, part * HGW + (m + 1) * 256))
    def sbcols(part, heads):
        base = 4 * HGW + part * 512
        out = []
        for h in heads:
            out += list(range(base + h * 64, base + (h + 1) * 64))
        return out
    def qk(m):
        return (sbcols(0, [4 * m, 4 * m + 1]) + sbcols(0, [4 * m + 2, 4 * m + 3]) +
                sbcols(1, [4 * m, 4 * m + 1]) + sbcols(1, [4 * m + 2, 4 * m + 3]))
    def v(m):
        return sbcols(2, [4 * m, 4 * m + 1, 4 * m + 2, 4 * m + 3])
    me, pa = j, 1 - j
    cols = (qk(me) + qk(pa) + v(me) + hgcols(0, me) + hgcols(2, me) + hgcols(3, me) +
            hgcols(1, me) + hgcols(1, pa) + v(pa) + hgcols(0, pa) + hgcols(2, pa) + hgcols(3, pa))
    assert len(cols) == 3584
    return np.array(cols)


def _mix_perm():
    rows = []
    for m in range(2):
        rows += list(range(m * 256, (m + 1) * 256))
    for m in range(2):
        rows += list(range(512 + m * 256, 512 + (m + 1) * 256))
    return np.array(rows)


def make_in_maps(inputs, S):
    T = S // 2
    f = lambda a: np.ascontiguousarray(np.asarray(a, dtype=np.float32))
    x = f(inputs["x"]); mem = f(inputs["mem"])
    B = x.shape[0]
    cb, cf = _consts()
    gains = np.stack([f(inputs["ffn1_norm"])[0], f(inputs["mix_norm"])[0], f(inputs["mem_q_norm"])[0],
                      f(inputs["mem_kv_norm"])[0], f(inputs["ffn2_norm"])[0], f(inputs["final_norm"])], axis=0)
    w_in = f(inputs["w_in"])[0]
    w_out = f(inputs["w_out"])[0][_mix_perm(), :]
    lbr = f(inputs["hg_lb_raw"]); hgn = f(inputs["hg_gnorm"])[0]; sbn = f(inputs["sb_gnorm"])[0]
    maps = []
    for c in range(2 * B):
        b, j = c // 2, c % 2
        hgp = np.stack([lbr[0, j * 256:(j + 1) * 256], lbr[1, j * 256:(j + 1) * 256], hgn[j * 256:(j + 1) * 256]], axis=0)
        sbg = np.ascontiguousarray(sbn[j * 256:(j + 1) * 256].reshape(4, 64).T)
        maps.append({
            "x": np.ascontiguousarray(x[b, j * T:(j + 1) * T]), "mem": np.ascontiguousarray(mem[b]),
            "w_gu1": f(inputs["ffn1_w_gu"])[0], "w_gu2": f(inputs["ffn2_w_gu"])[0],
            "w_d1": f(inputs["ffn1_w_down"])[0], "w_d2": f(inputs["ffn2_w_down"])[0],
            "w_in": np.ascontiguousarray(w_in[:, _win_perm(j)]), "w_out": np.ascontiguousarray(w_out),
            "w_q": f(inputs["mem_w_q"])[0], "w_kv": f(inputs["mem_w_kv"])[0], "w_o": f(inputs["mem_w_o"])[0],
            "gains": np.ascontiguousarray(gains), "hgp": np.ascontiguousarray(hgp), "sbg": sbg, "cb": cb, "cf": cf,
        })
    return maps


def kernel(**inputs):
    x = np.asarray(inputs["x"])
    B, S, _ = x.shape
    T = S // 2
    import os
    if S not in _CACHE:
        _CACHE[S] = build(S, stop=os.environ.get("K_STOP"))
    nc = _CACHE[S]
    maps = make_in_maps(inputs, S)
    res = run_bass_kernel_spmd(nc, maps, core_ids=list(range(2 * B)))
    out = np.empty((B, S, D), np.float32)
    for c in range(2 * B):
        out[c // 2, (c % 2) * T:(c % 2 + 1) * T] = res.results[c]["out"]
    return out
```
